# Optimizing a Trainium2 kernel written in Bass

```python
import math
import jax, jax.numpy as jnp
from jax import lax
import numpy as np

D_MODEL = 1024
BATCH = 8
SEQ = 4096
DEPTH = 2

GRID_W = 64
N_Q_HEADS = 16
N_KV_HEADS = 4
HEAD_DIM = 64
ATTN_WIDTH = N_Q_HEADS * HEAD_DIM
KV_WIDTH = N_KV_HEADS * HEAD_DIM
ROPE_THETA = 10000.0
Q_BLOCK = 128
SSM_EXPAND = 2
D_INNER = SSM_EXPAND * D_MODEL
SSM_HEAD_DIM = 64
N_SSM_HEADS = D_INNER // SSM_HEAD_DIM
N_SSM_GROUPS = 4
D_STATE = 128
D_CONV = 5
CHUNK = 128
CONV_DIM = D_INNER + 2 * N_SSM_GROUPS * D_STATE
MIX_WIDTH = ATTN_WIDTH + D_INNER
IN_PROJ_WIDTH = ATTN_WIDTH + 2 * KV_WIDTH + D_INNER + CONV_DIM + 2 * N_SSM_HEADS
D_FF = 4 * D_MODEL
NORM_EPS = 1e-5
QK_EPS = 1e-6

kernel_name = "hybrid_parallel_ssd_axial_gqa_encoder"


def rmsnorm(x, w, eps=NORM_EPS):
    xf = x.astype(jnp.float32)
    y = xf * lax.rsqrt(jnp.mean(xf * xf, axis=-1, keepdims=True) + eps)
    return (y * w.astype(jnp.float32)).astype(x.dtype)


def axial_rope_tables(seq):
    rows = seq // GRID_W
    row_ids = jnp.repeat(jnp.arange(rows, dtype=jnp.int32), GRID_W)
    col_ids = jnp.tile(jnp.arange(GRID_W, dtype=jnp.int32), rows)
    half = HEAD_DIM // 2
    inv_freq = ROPE_THETA ** (-jnp.arange(0, half, 2, dtype=jnp.float32) / half)

    def table(pos):
        ang = pos.astype(jnp.float32)[:, None] * inv_freq[None, :]
        ang = jnp.concatenate([ang, ang], axis=-1)
        return jnp.cos(ang), jnp.sin(ang)

    cr, sr = table(row_ids)
    cc, sc = table(col_ids)
    return cr, sr, cc, sc


def _rotate(x, cos, sin):
    x1, x2 = jnp.split(x, 2, axis=-1)
    rot = jnp.concatenate([-x2, x1], axis=-1)
    return x * cos[None, :, None, :] + rot * sin[None, :, None, :]


def axial_rope(x, tabs):
    cr, sr, cc, sc = tabs
    half = HEAD_DIM // 2
    xf = x.astype(jnp.float32)
    out = jnp.concatenate([_rotate(xf[..., :half], cr, sr), _rotate(xf[..., half:], cc, sc)], axis=-1)
    return out.astype(x.dtype)


def attention_group(q, k, v, q_norm_w, k_norm_w, tabs):
    b, s = q.shape[:2]
    rep = N_Q_HEADS // N_KV_HEADS
    q = axial_rope(rmsnorm(q.reshape(b, s, N_Q_HEADS, HEAD_DIM), q_norm_w, QK_EPS), tabs)
    k = axial_rope(rmsnorm(k.reshape(b, s, N_KV_HEADS, HEAD_DIM), k_norm_w, QK_EPS), tabs)
    v = v.reshape(b, s, N_KV_HEADS, HEAD_DIM)
    nb = s // Q_BLOCK
    qb = q.reshape(b, nb, Q_BLOCK, N_KV_HEADS, rep, HEAD_DIM).transpose(1, 0, 2, 3, 4, 5)
    scale = HEAD_DIM ** -0.5

    def one_block(qblk):
        sc = jnp.einsum("bqgrd,bkgd->bgrqk", qblk, k).astype(jnp.float32) * scale
        p = jax.nn.softmax(sc, axis=-1).astype(v.dtype)
        return jnp.einsum("bgrqk,bkgd->bqgrd", p, v)

    out = lax.map(one_block, qb)
    return out.transpose(1, 0, 2, 3, 4, 5).reshape(b, s, ATTN_WIDTH)


def centred_dwconv(u, w, bias):
    pad = D_CONV // 2
    out = lax.conv_general_dilated(
        u, w[:, None, :].astype(u.dtype), window_strides=(1,), padding=[(pad, pad)],
        dimension_numbers=("NWC", "WIO", "NWC"), feature_group_count=u.shape[-1])
    return out + bias.astype(u.dtype)


def ssd_chunked(xh, a, bm, cm):
    bsz, s, h, p = xh.shape
    g, n = bm.shape[2], bm.shape[3]
    e = h // g
    c = s // CHUNK
    x = xh.reshape(bsz, c, CHUNK, g, e, p)
    a = a.reshape(bsz, c, CHUNK, g, e).transpose(0, 3, 4, 1, 2)
    B = bm.reshape(bsz, c, CHUNK, g, n)
    C = cm.reshape(bsz, c, CHUNK, g, n)
    a_cs = jnp.cumsum(a, axis=-1)
    seg = a_cs[..., :, None] - a_cs[..., None, :]
    mask = jnp.tril(jnp.ones((CHUNK, CHUNK), dtype=bool))
    lmat = jnp.exp(jnp.where(mask, seg, -jnp.inf))
    cb = jnp.einsum("bclgn,bcsgn->bgcls", C, B)
    y_diag = jnp.einsum("bgcls,bgecls,bcsgep->bclgep", cb, lmat, x)
    decay_states = jnp.exp(a_cs[..., -1:] - a_cs)
    states = jnp.einsum("bclgn,bgecl,bclgep->bcgepn", B, decay_states, x)
    chunk_decay = jnp.moveaxis(jnp.exp(a_cs[..., -1]), 3, 0)

    def step(carry, inp):
        st, dec = inp
        return carry * dec[..., None, None] + st, carry

    init = jnp.zeros((bsz, g, e, p, n), dtype=jnp.float32)
    _, prev = lax.scan(step, init, (jnp.moveaxis(states, 1, 0), chunk_decay))
    y_off = jnp.einsum("bclgn,cbgepn,bgecl->bclgep", C, prev, jnp.exp(a_cs))
    return (y_diag + y_off).reshape(bsz, s, h, p)


def ssd_group(z, xbc, dt_raw, conv_w, conv_b, dt_bias_f, dt_bias_b, a_log_f, a_log_b, d_skip, norm_w):
    bsz, s = z.shape[:2]
    f32 = jnp.float32
    xbc = jax.nn.silu(centred_dwconv(xbc, conv_w, conv_b)).astype(f32)
    gn = N_SSM_GROUPS * D_STATE
    xs = xbc[..., :D_INNER].reshape(bsz, s, N_SSM_HEADS, SSM_HEAD_DIM)
    bm = xbc[..., D_INNER:D_INNER + gn].reshape(bsz, s, N_SSM_GROUPS, D_STATE)
    cm = xbc[..., D_INNER + gn:].reshape(bsz, s, N_SSM_GROUPS, D_STATE)
    dt_raw = dt_raw.astype(f32)
    dt_f = jax.nn.softplus(dt_raw[..., :N_SSM_HEADS] + dt_bias_f.astype(f32))
    dt_b = jax.nn.softplus(dt_raw[..., N_SSM_HEADS:] + dt_bias_b.astype(f32))
    A_f = -jnp.exp(a_log_f.astype(f32))
    A_b = -jnp.exp(a_log_b.astype(f32))
    flip = lambda t: jnp.flip(t, axis=1)
    y_f = ssd_chunked(xs * dt_f[..., None], dt_f * A_f, bm, cm)
    y_b = flip(ssd_chunked(flip(xs * dt_b[..., None]), flip(dt_b * A_b), flip(bm), flip(cm)))
    y = y_f + y_b + d_skip.astype(f32)[:, None] * xs
    y = y.reshape(bsz, s, D_INNER) * jax.nn.silu(z.astype(f32))
    yg = y.reshape(bsz, s, N_SSM_GROUPS, D_INNER // N_SSM_GROUPS)
    yg = yg * lax.rsqrt(jnp.mean(yg * yg, axis=-1, keepdims=True) + NORM_EPS)
    y = yg.reshape(bsz, s, D_INNER) * norm_w.astype(f32)
    return y.astype(z.dtype)


def hybrid_layer(x, ln1_w, w_in, conv_w, conv_b, dt_bias_f, dt_bias_b, a_log_f, a_log_b, d_skip,
                 ssm_norm_w, q_norm_w, k_norm_w, w_out, ln2_w, w_up, w_down, tabs):
    h = rmsnorm(x, ln1_w)
    proj = h @ w_in
    i0 = ATTN_WIDTH
    i1 = i0 + KV_WIDTH
    i2 = i1 + KV_WIDTH
    i3 = i2 + D_INNER
    i4 = i3 + CONV_DIM
    q, k, v, z, xbc, dt_raw = jnp.split(proj, [i0, i1, i2, i3, i4], axis=-1)
    attn = attention_group(q, k, v, q_norm_w, k_norm_w, tabs)
    ssm = ssd_group(z, xbc, dt_raw, conv_w, conv_b, dt_bias_f, dt_bias_b, a_log_f, a_log_b,
                    d_skip, ssm_norm_w)
    x = x + jnp.concatenate([attn, ssm], axis=-1) @ w_out
    h = rmsnorm(x, ln2_w)
    x = x + jnp.square(jax.nn.relu(h @ w_up)) @ w_down
    return x


def setup_inputs(seed: int = 0) -> dict:
    key = jax.random.key(seed)
    ks = jax.random.split(key, 20)
    f32 = jnp.float32
    L = DEPTH
    nrm = lambda k, shape, scale: jax.random.normal(k, shape, f32) * scale
    gain = lambda k, shape: 1.0 + 0.02 * jax.random.normal(k, shape, f32)
    dt_lo, dt_hi = 1e-3, 1e-1

    def dt_bias_init(k):
        u = jax.random.uniform(k, (L, N_SSM_HEADS), f32)
        dt = jnp.exp(u * (math.log(dt_hi) - math.log(dt_lo)) + math.log(dt_lo))
        return dt + jnp.log(-jnp.expm1(-dt))

    def a_log_init(k):
        return jnp.log(jax.random.uniform(k, (L, N_SSM_HEADS), f32, 1.0, 16.0))

    return {
        "x": jax.random.normal(ks[0], (BATCH, SEQ, D_MODEL), f32),
        "ln1_w": gain(ks[1], (L, D_MODEL)),
        "w_in": nrm(ks[2], (L, D_MODEL, IN_PROJ_WIDTH), D_MODEL ** -0.5),
        "conv_w": nrm(ks[3], (L, D_CONV, CONV_DIM), D_CONV ** -0.5),
        "conv_b": nrm(ks[4], (L, CONV_DIM), 0.02),
        "dt_bias_fwd": dt_bias_init(ks[5]),
        "dt_bias_bwd": dt_bias_init(ks[6]),
        "a_log_fwd": a_log_init(ks[7]),
        "a_log_bwd": a_log_init(ks[8]),
        "d_skip": gain(ks[9], (L, N_SSM_HEADS)),
        "ssm_norm_w": gain(ks[10], (L, D_INNER)),
        "q_norm_w": gain(ks[11], (L, HEAD_DIM)),
        "k_norm_w": gain(ks[12], (L, HEAD_DIM)),
        "w_out": nrm(ks[13], (L, MIX_WIDTH, D_MODEL), MIX_WIDTH ** -0.5),
        "ln2_w": gain(ks[14], (L, D_MODEL)),
        "w_up": nrm(ks[15], (L, D_MODEL, D_FF), D_MODEL ** -0.5),
        "w_down": nrm(ks[16], (L, D_FF, D_MODEL), D_FF ** -0.5),
        "final_norm_w": gain(ks[17], (D_MODEL,)),
    }


def reference(x, ln1_w, w_in, conv_w, conv_b, dt_bias_fwd, dt_bias_bwd, a_log_fwd, a_log_bwd,
              d_skip, ssm_norm_w, q_norm_w, k_norm_w, w_out, ln2_w, w_up, w_down, final_norm_w):
    tabs = axial_rope_tables(x.shape[1])
    for i in range(DEPTH):
        x = hybrid_layer(x, ln1_w[i], w_in[i], conv_w[i], conv_b[i], dt_bias_fwd[i], dt_bias_bwd[i],
                         a_log_fwd[i], a_log_bwd[i], d_skip[i], ssm_norm_w[i], q_norm_w[i],
                         k_norm_w[i], w_out[i], ln2_w[i], w_up[i], w_down[i], tabs)
    return rmsnorm(x, final_norm_w)
```

```python
import numpy as np
from contextlib import ExitStack
import ml_dtypes
import concourse.bass as bass
import concourse.mybir as mybir
from concourse.bass_utils import run_bass_kernel_spmd

F32 = mybir.dt.float32
BF16 = mybir.dt.bfloat16
AF = mybir.ActivationFunctionType
ALU = mybir.AluOpType
AX = mybir.AxisListType

D = 1024
S = 4096
NT = S // 128
DEPTH = 2
NQ, NKV, HD = 16, 4, 64
DI = 2048
NH = 32
NG = 4
DS = 128
CONV = 3072
WIN = 6720
DFF = 4096
NORM_EPS = 1e-5
QK_EPS = 1e-6
BIG = 30000.0

ENGS = ("pe", "act", "dve", "pool", "sp")
DBG = {}


class DmaSem:
    def __init__(self, nc, name):
        self.h = nc.alloc_semaphore(name)
        self.n = 0


class Op:
    __slots__ = ("eng", "fn", "deps", "cdeps", "marked", "is_dma", "sem", "val", "idx")

    def __init__(self, eng, fn, is_dma):
        self.eng = eng
        self.fn = fn
        self.deps = []
        self.cdeps = {}
        self.marked = False
        self.is_dma = is_dma
        self.sem = None
        self.val = 0
        self.idx = 0


class Planner:
    def __init__(self, nc):
        self.nc = nc
        self.eng_sem = {e: nc.alloc_semaphore("es_" + e) for e in ("pe", "act", "dve", "pool")}
        self.eng_cnt = {e: 0 for e in ("pe", "act", "dve", "pool")}
        self.dma_sems = {}
        self.free_dsems = {"sw": [], "hw": []}
        self.n_dsems = 0
        self.reset()

    def reset(self):
        self.ops = {e: [] for e in ENGS}
        self.last_w = {}
        self.readers = {}
        self.nops = 0

    def dsem(self, name, q="sp"):
        kind = "sw" if q == "pool" else "hw"
        if name not in self.dma_sems:
            fl = self.free_dsems[kind]
            if fl:
                self.dma_sems[name] = fl.pop()
            else:
                self.dma_sems[name] = DmaSem(self.nc, f"ds{kind}{self.n_dsems}")
                self.dma_sems[name].kind = kind
                self.n_dsems += 1
        assert self.dma_sems[name].kind == kind, name
        return self.dma_sems[name]

    def _add_dep(self, op, d, raw):
        if d is op:
            return
        if d.eng == op.eng and not d.is_dma and not op.is_dma:
            if op.eng == "pe":
                return
        if d.is_dma and op.is_dma and d.sem is op.sem:
            return
        if d.is_dma:
            if d not in op.deps:
                op.deps.append(d)
            return
        cur = op.cdeps.get(d.eng)
        if cur is None or d.idx > cur.idx:
            op.cdeps[d.eng] = d

    def add(self, eng, fn, r=(), w=(), dsem=None):
        op = Op(eng, fn, dsem is not None)
        op.idx = self.nops
        self.nops += 1
        if dsem is not None:
            dsem.n += 16
            op.sem = dsem.h
            op.val = dsem.n
        for k in r:
            lw = self.last_w.get(k)
            if lw is not None:
                self._add_dep(op, lw, True)
            if isinstance(k, str) and k.startswith("ps"):
                for rd in self.readers.get(k, ()):
                    if rd.eng != op.eng:
                        self._add_dep(op, rd, False)
        for k in w:
            lw = self.last_w.get(k)
            if lw is not None:
                self._add_dep(op, lw, False)
            for rd in self.readers.get(k, ()):
                self._add_dep(op, rd, False)
        for k in r:
            self.readers.setdefault(k, []).append(op)
        for k in w:
            self.last_w[k] = op
            self.readers[k] = []
        self.ops[eng].append(op)
        return op

    def dma(self, q, out, in_, r=(), w=(), sem=None):
        return self.add(q, lambda e: e.dma_start(out=out, in_=in_), r, w, dsem=self.dsem(sem, q))

    def mm(self, out, lhsT, rhs, start=True, stop=True, r=(), w=()):
        return self.add("pe", lambda e: e.matmul(out, lhsT=lhsT, rhs=rhs, start=start, stop=stop), r, w)

    def tr(self, out, in_, ident, r=(), w=()):
        return self.add("pe", lambda e: e.transpose(out, in_, ident), r, w)

    def act(self, out, in_, func, bias=None, scale=None, accum_out=None, r=(), w=()):
        kw = {}
        if bias is not None:
            kw["bias"] = bias
        if scale is not None:
            kw["scale"] = scale
        if accum_out is not None:
            kw["accum_out"] = accum_out
        return self.add("act", lambda e: e.activation(out=out, in_=in_, func=func, **kw), r, w)

    def tt(self, eng, out, in0, in1, op, r=(), w=()):
        return self.add(eng, lambda e: e.tensor_tensor(out=out, in0=in0, in1=in1, op=op), r, w)

    def ts(self, eng, out, in0, s1, op0, s2=None, op1=None, r=(), w=()):
        if op1 is None:
            return self.add(eng, lambda e: e.tensor_scalar(out=out, in0=in0, scalar1=s1, scalar2=None, op0=op0), r, w)
        return self.add(eng, lambda e: e.tensor_scalar(out=out, in0=in0, scalar1=s1, scalar2=s2, op0=op0, op1=op1), r, w)

    def stt(self, eng, out, in0, scalar, in1, op0, op1, r=(), w=()):
        return self.add(eng, lambda e: e.scalar_tensor_tensor(out=out, in0=in0, scalar=scalar, in1=in1, op0=op0, op1=op1), r, w)

    def copy(self, eng, out, in_, r=(), w=()):
        if eng == "act":
            return self.add("act", lambda e: e.activation(out=out, in_=in_, func=AF.Copy), r, w)
        return self.add(eng, lambda e: e.tensor_copy(out=out, in_=in_), r, w)

    def memset(self, eng, ap, val, w=()):
        return self.add(eng, lambda e: e.memset(ap, val), (), w)

    def recip(self, out, in_, r=(), w=()):
        return self.add("dve", lambda e: e.reciprocal(out=out, in_=in_), r, w)

    def emit(self, name):
        nc = self.nc
        for e in ENGS:
            for op in self.ops[e]:
                for d in op.cdeps.values():
                    d.marked = True
                    op.deps.append(d)
        for e in ("pe", "act", "dve", "pool"):
            cnt = self.eng_cnt[e]
            for op in self.ops[e]:
                if not op.is_dma and op.marked:
                    cnt += 1
                    op.sem = self.eng_sem[e]
                    op.val = cnt
            self.eng_cnt[e] = cnt
        final_dma = {}
        for e in ENGS:
            for op in self.ops[e]:
                if op.is_dma:
                    final_dma[op.sem] = max(final_dma.get(op.sem, 0), op.val)
        ops = self.ops

        def body(eobj, lst, is_sp):
            waited = {}
            for op in lst:
                for d in op.deps:
                    if waited.get(d.sem, 0) >= d.val:
                        continue
                    eobj.wait_ge(d.sem, d.val)
                    waited[d.sem] = d.val
                ins = op.fn(eobj)
                if op.is_dma:
                    ins.then_inc(op.sem, 16)
                elif op.marked:
                    ins.then_inc(op.sem, 1)
            if is_sp:
                for sem, val in final_dma.items():
                    eobj.wait_ge(sem, val)

        with nc.Block(name) as block:
            block.tensor(lambda e: body(e, ops["pe"], False))
            block.scalar(lambda e: body(e, ops["act"], False))
            block.vector(lambda e: body(e, ops["dve"], False))
            block.gpsimd(lambda e: body(e, ops["pool"], False))
            block.sync(lambda e: body(e, ops["sp"], True))
        for ds in self.dma_sems.values():
            self.free_dsems[ds.kind].append(ds)
        self.dma_sems = {}
        self.reset()


class Ctx:
    def __init__(self, nc):
        self.nc = nc
        self.P = Planner(nc)
        self.dram = {}
        self.bank = 0

    def dt(self, name, shape, dtype, kind=None):
        if kind is None:
            t = self.nc.dram_tensor(name, list(shape), dtype)
        else:
            t = self.nc.dram_tensor(name, list(shape), dtype, kind=kind)
        self.dram[name] = t.ap()
        return self.dram[name]

    def next_bank(self):
        b = self.bank
        self.bank = (b + 1) % 8
        return b


def rope_tables():
    t = np.arange(S)
    row = (t // 64).astype(np.float32)
    col = (t % 64).astype(np.float32)
    half = HD // 2
    inv = (np.float32(10000.0) ** (-np.arange(0, half, 2, dtype=np.float32) / np.float32(half))).astype(np.float32)

    def tab(pos):
        ang = pos[:, None] * inv[None, :]
        ang = np.concatenate([ang, ang], axis=-1).astype(np.float32)
        return np.cos(ang).astype(np.float32), np.sin(ang).astype(np.float32)

    cr, sr = tab(row)
    cc, sc = tab(col)
    cos = np.concatenate([cr, cc], axis=-1)
    sin = np.concatenate([sr, sc], axis=-1)
    sgn = np.tile(np.concatenate([-np.ones(16, np.float32), np.ones(16, np.float32)]), 2)
    sinS = sin * sgn[None, :]
    swap = np.arange(64).reshape(2, 2, 16)[:, ::-1, :].reshape(64)
    sinSw = sinS[:, swap]
    return np.ascontiguousarray(cos), np.ascontiguousarray(sinSw)


def const_layout():
    off = {}
    o = 0

    def put(name, n):
        nonlocal o
        off[name] = (o, n)
        o += n

    for L in range(DEPTH):
        put(f"ln1_{L}", 8)
        put(f"ln2_{L}", 8)
        put(f"qw_{L}", 64)
        put(f"kw_{L}", 64)
        put(f"dtb_{L}", 64)
        put(f"alog_{L}", 64)
        put(f"convw_{L}", 24 * 5)
        put(f"convb_{L}", 24)
        put(f"dskip_{L}", 32)
        put(f"ssmw_{L}", 16)
    put("fnw", 1024)
    return off, o


def pack_consts(inp):
    off, n = const_layout()
    c = np.zeros((128, n), np.float32)

    def setc(name, arr):
        o, m = off[name]
        c[:, o:o + m] = arr.reshape(128, m)

    rep = lambda v: np.broadcast_to(np.asarray(v, np.float32).reshape(1, -1), (128, np.asarray(v).size))
    for L in range(DEPTH):
        setc(f"ln1_{L}", np.asarray(inp["ln1_w"][L]).reshape(8, 128).T)
        setc(f"ln2_{L}", np.asarray(inp["ln2_w"][L]).reshape(8, 128).T)
        setc(f"qw_{L}", rep(inp["q_norm_w"][L]))
        setc(f"kw_{L}", rep(inp["k_norm_w"][L]))
        setc(f"dtb_{L}", rep(np.concatenate([inp["dt_bias_fwd"][L], inp["dt_bias_bwd"][L]])))
        setc(f"alog_{L}", rep(np.concatenate([inp["a_log_fwd"][L], inp["a_log_bwd"][L]])))
        cw = np.asarray(inp["conv_w"][L]).reshape(5, 24, 128)
        setc(f"convw_{L}", np.ascontiguousarray(cw.transpose(2, 1, 0)))
        setc(f"convb_{L}", np.asarray(inp["conv_b"][L]).reshape(24, 128).T)
        setc(f"dskip_{L}", rep(inp["d_skip"][L]))
        setc(f"ssmw_{L}", np.asarray(inp["ssm_norm_w"][L]).reshape(16, 128).T)
    setc("fnw", rep(inp["final_norm_w"]))
    return c


def fixed_consts():
    ident = np.eye(128, dtype=np.float32)
    j = np.arange(128)[:, None]
    l = np.arange(128)[None, :]
    U = (j <= l).astype(np.float32)
    Lm = (j >= l).astype(np.float32)
    ones = np.ones((128, 128), np.float32)
    tri = np.concatenate([ident, U, Lm, ones], axis=1)
    mf = np.where(l < j, -BIG, 0.0).astype(np.float32)
    mb = np.where(l > j, -BIG, 0.0).astype(np.float32)
    ind = np.zeros((64, 32, 128), np.float32)
    for hl in range(2):
        for h in range(32):
            ind[hl * 32 + h, h, :] = 1.0
    bfc = np.zeros((128, 128 + 128 + 128 + 4096), np.float32)
    bfc[:, 0:128] = ident
    bfc[:, 128:256] = mf
    bfc[:, 256:384] = mb
    bfc[0:64, 384:384 + 4096] = ind.reshape(64, 4096)
    bfc[64:128, 384:384 + 4096] = 1.0
    return tri, bfc.astype(ml_dtypes.bfloat16)


def phase_a1(C, L, x_src):
    nc, P, dr = C.nc, C.P, C.dram
    coff, _ = const_layout()
    with ExitStack() as es:
        def sb(name, shape, dt):
            return es.enter_context(nc.sbuf_tensor(f"a1{L}_{name}", shape, dt))

        ps = es.enter_context(nc.psum_tensor(f"a1{L}_ps", [128, 6, 512], F32))
        psb = es.enter_context(nc.psum_tensor(f"a1{L}_psb", [128, 2, 1024], BF16))
        cst = sb("cst", [128, const_layout()[1]], F32)
        tri = sb("tri", [128, 512], F32)
        identb = sb("identb", [128, 128], BF16)
        wbf = sb("wbf", [128, 8, 3648], BF16)
        xs = [sb(f"x{i}", [128, 1024], F32) for i in range(3)]
        junk = sb("junk", [128, 1024], BF16)
        ss = [sb(f"ss{i}", [128, 4], F32) for i in range(2)]
        hbf = [sb(f"h{i}", [128, 1024], BF16) for i in range(2)]
        hT = [sb(f"hT{i}", [128, 8, 512], BF16) for i in range(2)]
        cosb = [sb(f"cos{i}", [128, 64], F32) for i in range(3)]
        sinb = [sb(f"sin{i}", [128, 64], F32) for i in range(3)]
        cw = [sb(f"cw{i}", [128, 2, 64], F32) for i in range(2)]
        sw = [sb(f"sw{i}", [128, 2, 64], F32) for i in range(2)]
        sq = sb("sq", [128, 512], F32)
        ssq = sb("ssq", [128, 3, 8], F32)
        rstd = sb("rstd", [128, 3, 8], F32)
        xn = sb("xn", [128, 512], F32)
        t1 = sb("t1", [128, 512], F32)
        u = sb("u", [128, 512], F32)
        kbf = sb("kbf", [128, 256], BF16)
        qkbf = [sb(f"qkbf{i}", [128, 12, 128], BF16) for i in range(2)]
        qTs = [sb(f"qTs{i}", [128, 8, 512], BF16) for i in range(2)]
        kTs = [sb(f"kTs{i}", [128, 4, 512], BF16) for i in range(2)]
        vst = [sb(f"vst{i}", [128, 4, 256], BF16) for i in range(2)]
        zst = [sb(f"zst{i}", [128, 2048], BF16) for i in range(2)]
        abc = sb("abc", [128, 64], F32)
        dtb = sb("dtb", [128, 64], F32)
        e1 = sb("e1", [128, 64], F32)
        dtv = sb("dtv", [128, 64], F32)
        lndt = sb("lndt", [128, 64], F32)
        a32 = sb("a32", [128, 64], F32)
        csn = sb("csn", [128, 128], F32)
        dfx = sb("dfx", [128, 64], F32)
        exx = sb("exx", [128, 64], F32)
        scal = [sb(f"scal{i}", [128, 128], F32) for i in range(2)]
        cdt = [sb(f"cdt{i}", [128, 64], F32) for i in range(2)]
        trh = [sb(f"trh{i}", [128, 128], BF16) for i in range(2)]
        trl = [sb(f"trl{i}", [128, 128], BF16) for i in range(2)]
        epsb = sb("epsb", [128, 4], F32)
        P.memset("dve", epsb[:, 0:1], NORM_EPS, w=["epsb"])
        P.memset("dve", epsb[:, 1:2], QK_EPS, w=["epsb"])
        P.memset("dve", epsb[:, 2:3], 1.0, w=["epsb"])

        def cs_(name):
            o, n = coff[name]
            return cst[:, o:o + n]

        P.dma("sp", cst[:], dr["consts"], w=["cst"], sem="a1cst")
        P.dma("sp", tri[:], dr["tri"], w=["tri"], sem="a1tri")
        P.dma("sp", identb[:], dr["bfc"][:, 0:128], w=["identb"], sem="a1id")
        win = dr["w_in"]
        for cbk in range(8):
            c0 = cbk * 512
            n = 512 if cbk < 7 else 64
            src0 = c0 if cbk < 7 else 6656
            for k in range(8):
                P.dma("pool", wbf[:, k, c0:c0 + n], win[L, k * 128:(k + 1) * 128, src0:src0 + n], w=[f"wb{cbk}"], sem=f"a1wb{cbk}")
        P.act(abc[:], cs_(f"alog_{L}"), AF.Exp, r=["cst"], w=["abc"])
        P.ts("dve", abc[:], abc[:], -1.0, ALU.mult, r=["abc"], w=["abc"])

        ident_f = tri[:, 0:128]
        fb = 0
        bb = 0

        def nb():
            nonlocal fb
            b = fb
            fb = (fb + 1) % 6
            return b

        def nbb():
            nonlocal bb
            b = bb
            bb = (bb + 1) % 2
            return b

        NTT = DBG.get("a1_tiles", NT)

        def loads(t):
            x3 = t % 3
            P.dma("sp", xs[x3][:], x_src[t * 128:(t + 1) * 128, :], w=[f"x{x3}"], sem=f"a1x{x3}")
            P.dma("sp", cosb[x3][:], dr["rope_cos"][t * 128:(t + 1) * 128, :], w=[f"cos{x3}"], sem=f"a1cos{x3}")
            P.dma("sp", sinb[x3][:], dr["rope_sin"][t * 128:(t + 1) * 128, :], w=[f"sin{x3}"], sem=f"a1sin{x3}")

        def front_a(t):
            s2 = t % 2
            ch = t // 4
            cs2 = ch % 2
            j = t % 4
            x3 = t % 3
            if t == 0:
                loads(0)
                loads(1)
            if t + 2 < NTT:
                loads(t + 2)
            P.act(junk[:], xs[x3][:], AF.Square, accum_out=ss[s2][:, 0:1], r=[f"x{x3}"], w=["junk", f"ss{s2}"])
            P.act(ss[s2][:, 1:2], ss[s2][:, 0:1], AF.Ln, scale=1.0 / D, bias=epsb[:, 0:1], r=[f"ss{s2}", "epsb"], w=[f"ssb{s2}"])
            P.act(ss[s2][:, 2:3], ss[s2][:, 1:2], AF.Exp, scale=-0.5, r=[f"ssb{s2}"], w=[f"ssc{s2}"])
            P.act(hbf[s2][:], xs[x3][:], AF.Copy, scale=ss[s2][:, 2:3], r=[f"x{x3}", f"ssc{s2}"], w=[f"h{s2}"])
            P.tt("dve", cw[s2][:, 0, :], cosb[x3][:], cs_(f"qw_{L}"), ALU.mult, r=[f"cos{x3}", "cst"], w=[f"cw{s2}"])
            P.tt("dve", cw[s2][:, 1, :], cosb[x3][:], cs_(f"kw_{L}"), ALU.mult, r=[f"cos{x3}", "cst"], w=[f"cw{s2}"])
            P.tt("dve", sw[s2][:, 0, :], sinb[x3][:], cs_(f"qw_{L}"), ALU.mult, r=[f"sin{x3}", "cst"], w=[f"sw{s2}"])
            P.tt("dve", sw[s2][:, 1, :], sinb[x3][:], cs_(f"kw_{L}"), ALU.mult, r=[f"sin{x3}", "cst"], w=[f"sw{s2}"])

        def front_b(t):
            s2 = t % 2
            ch = t // 4
            cs2 = ch % 2
            j = t % 4
            b = nbb()
            for k in range(8):
                P.tr(psb[:, b, k * 128:(k + 1) * 128], hbf[s2][:, k * 128:(k + 1) * 128], identb[:],
                     r=[f"h{s2}", "identb"], w=[f"psb{b}"])
            lnw = cs_(f"ln1_{L}")
            P.tt("dve", hT[cs2][:, :, j * 128:(j + 1) * 128],
                 psb[:, b, :].rearrange("p (k t) -> p k t", k=8),
                 lnw.unsqueeze(2).to_broadcast([128, 8, 128]), ALU.mult,
                 r=[f"psb{b}", "cst"], w=[f"hT{cs2}_{j}"])

        def back(t):
            s2 = t % 2
            ch = t // 4
            cs2 = ch % 2
            j = t % 4
            hTk = lambda k: hT[cs2][:, k, j * 128:(j + 1) * 128]
            hkey = f"hT{cs2}_{j}"
            do_ssd = "ssd" not in DBG.get("skip", ())
            if t + 1 < NTT:
                front_a(t + 1)
            def proj(c0, n):
                bk = nb()
                for k in range(8):
                    P.mm(ps[:, bk, 0:n], hTk(k), wbf[:, k, c0:c0 + n], start=(k == 0), stop=(k == 7),
                         r=[hkey, f"wb{c0 // 512}"], w=[f"ps{bk}"])
                return bk

            for gi in range(0 if "qk" in DBG.get("skip", ()) else 3):
                nh = 8 if gi < 2 else 4
                wi = 0 if gi < 2 else 1
                bk = proj(gi * 512, 512)
                if gi == 0 and t >= 1 and do_ssd:
                    ssd2a(t - 1)
                n = nh * 64
                X = ps[:, bk, 0:n]
                QS = DBG.get("qk_stop", 99)
                P.act(sq[:, 0:n], X, AF.Square, r=[f"ps{bk}"], w=["sq"])
                if QS < 2:
                    return
                P.add("dve", lambda e, o=ssq[:, gi, 0:nh], i=sq[:, 0:n].rearrange("p (h d) -> p h d", d=64):
                      e.tensor_reduce(out=o, in_=i, axis=AX.X, op=ALU.add), r=["sq"], w=["ssq"])
                if QS < 3:
                    return
                P.act(rstd[:, gi, 0:nh], ssq[:, gi, 0:nh], AF.Ln, scale=1.0 / HD, bias=epsb[:, 1:2], r=["ssq", "epsb"], w=["rstd"])
                P.act(rstd[:, gi, 0:nh], rstd[:, gi, 0:nh], AF.Exp, scale=-0.5, r=["rstd"], w=["rstd"])
                if QS < 4:
                    return
                P.tt("dve", xn[:, 0:n].rearrange("p (h d) -> p h d", d=64), X.rearrange("p (h d) -> p h d", d=64),
                     rstd[:, gi, 0:nh].unsqueeze(2).to_broadcast([128, nh, 64]), ALU.mult,
                     r=[f"ps{bk}", "rstd"], w=["xn"])
                if QS < 5:
                    return
                P.tt("dve", t1[:, 0:n].rearrange("p (h d) -> p h d", d=64), xn[:, 0:n].rearrange("p (h d) -> p h d", d=64),
                     cw[s2][:, wi, :].unsqueeze(1).to_broadcast([128, nh, 64]), ALU.mult, r=["xn", f"cw{s2}"], w=["t1"])
                P.tt("dve", u[:, 0:n].rearrange("p (h d) -> p h d", d=64), xn[:, 0:n].rearrange("p (h d) -> p h d", d=64),
                     sw[s2][:, wi, :].unsqueeze(1).to_broadcast([128, nh, 64]), ALU.mult, r=["xn", f"sw{s2}"], w=["u"])
                if QS < 6:
                    return
                t1v = t1[:, 0:n].rearrange("p (a b d) -> p a b d", b=2, d=16)
                uv = u[:, 0:n].rearrange("p (a b d) -> p a b d", b=2, d=16)
                if gi < 2:
                    ov = qkbf[s2][:, gi * 4:(gi + 1) * 4, :].rearrange("p a (x b d) -> p (a x) b d", b=2, d=16)
                    okey = f"qkbf{s2}"
                else:
                    ov = kbf[:].rearrange("p (a b d) -> p a b d", b=2, d=16)
                    okey = "kbf"
                if "adds" not in DBG.get("skip", ()):
                    P.tt("dve", ov[:, :, 0, :], t1v[:, :, 0, :], uv[:, :, 1, :], ALU.add, r=["t1", "u"], w=[okey])
                    P.tt("dve", ov[:, :, 1, :], t1v[:, :, 1, :], uv[:, :, 0, :], ALU.add, r=["t1", "u"], w=[okey])
                if gi == 2 and "kdup" not in DBG.get("skip", ()):
                    kd = qkbf[s2][:, 8:12, :]
                    P.copy("pool", kd[:, :, 0:64], kbf[:].rearrange("p (g d) -> p g d", d=64), r=["kbf"], w=[f"qkbf{s2}"])
                    P.copy("pool", kd[:, :, 64:128], kbf[:].rearrange("p (g d) -> p g d", d=64), r=["kbf"], w=[f"qkbf{s2}"])
                if gi == 2 and "vcopy" not in DBG.get("skip", ()):
                    P.copy("act", vst[cs2][:, j, :], ps[:, bk, 256:512], r=[f"ps{bk}"], w=[f"vst{cs2}"])
            if t + 1 < NTT:
                front_b(t + 1)
            if t >= 1 and do_ssd:
                ssd2b(t - 1)
            for zi in range(0 if "z" in DBG.get("skip", ()) else 4):
                bk = proj(1536 + zi * 512, 512)
                P.act(zst[s2][:, zi * 512:(zi + 1) * 512], ps[:, bk, :], AF.Silu, r=[f"ps{bk}"], w=[f"zst{s2}"])
            P.dma("sp", dr["sz"][t * 128:(t + 1) * 128, :], zst[s2][:], r=[f"zst{s2}"], sem=f"a1zst{s2}")
            if "qk" in DBG.get("skip", ()) or DBG.get("qk_stop", 99) < 7:
                return
            b0 = nbb()
            for i in range(8):
                P.tr(psb[:, b0, i * 128:(i + 1) * 128], qkbf[s2][:, i, :], identb[:], r=[f"qkbf{s2}", "identb"], w=[f"psb{b0}"])
            P.copy("act", qTs[cs2][:, :, j * 128:(j + 1) * 128], psb[:, b0, :].rearrange("p (k t) -> p k t", k=8),
                   r=[f"psb{b0}"], w=[f"qTs{cs2}"])
            b1 = nbb()
            for i in range(4):
                P.tr(psb[:, b1, i * 128:(i + 1) * 128], qkbf[s2][:, 8 + i, :], identb[:], r=[f"qkbf{s2}", "identb"], w=[f"psb{b1}"])
            P.copy("act", kTs[cs2][:, :, j * 128:(j + 1) * 128], psb[:, b1, 0:512].rearrange("p (k t) -> p k t", k=4),
                   r=[f"psb{b1}"], w=[f"kTs{cs2}"])
            if j == 3:
                c0 = ch * 512
                P.dma("sp", dr["hT"].rearrange("(k p) t -> p k t", p=128)[:, :, c0:c0 + 512], hT[cs2][:],
                      r=[f"hT{cs2}_{jj}" for jj in range(4)], sem=f"a1hTst{cs2}")
                P.dma("sp", dr["qT"].rearrange("k p t -> p k t")[:, :, c0:c0 + 512], qTs[cs2][:], r=[f"qTs{cs2}"], sem=f"a1qTs{cs2}")
                P.dma("sp", dr["kT"].rearrange("k p t -> p k t")[:, :, c0:c0 + 512], kTs[cs2][:], r=[f"kTs{cs2}"], sem=f"a1kTs{cs2}")
                P.dma("sp", dr["v"][c0:c0 + 512, :].rearrange("(j p) c -> p j c", p=128), vst[cs2][:], r=[f"vst{cs2}"], sem=f"a1vst{cs2}")
            if "ssd" in DBG.get("skip", ()):
                return
            bk = proj(3584, 64)
            P.tt("dve", dtb[:], ps[:, bk, 0:64], cs_(f"dtb_{L}"), ALU.add, r=[f"ps{bk}", "cst"], w=["dtb"])
            P.act(e1[:], dtb[:], AF.Exp, r=["dtb"], w=["e1"])
            P.act(dtv[:], e1[:], AF.Ln, bias=epsb[:, 2:3], r=["e1", "epsb"], w=["dtv"])
            P.act(lndt[:], dtv[:], AF.Ln, r=["dtv"], w=["lndt"])
            P.tt("dve", a32[:], dtv[:], abc[:], ALU.mult, r=["dtv", "abc"], w=["a32"])

        def ssd2a(t):
            s2 = t % 2
            bk = nb()
            for i in range(3):
                P.mm(ps[:, bk, i * 64:(i + 1) * 64], tri[:, (i + 1) * 128:(i + 2) * 128], a32[:], r=["tri", "a32"], w=[f"ps{bk}"])
            P.copy("dve", csn[:, 0:32], ps[:, bk, 0:32], r=[f"ps{bk}"], w=["csn"])
            P.copy("dve", csn[:, 32:64], ps[:, bk, 64 + 32:64 + 64], r=[f"ps{bk}"], w=["csn"])
            P.tt("dve", dfx[:], ps[:, bk, 128:192], csn[:, 0:64], ALU.subtract, r=[f"ps{bk}", "csn"], w=["dfx"])
            P.act(cdt[s2][:], ps[:, bk, 128:192], AF.Exp, r=[f"ps{bk}"], w=[f"cdt{s2}"])
            P.dma("sp", dr["cd"][t], cdt[s2][:], r=[f"cdt{s2}"], sem=f"a1cdt{s2}")
            P.act(scal[s2][:, 0:64], csn[:, 0:64], AF.Exp, r=["csn"], w=[f"scal{s2}"])
            P.act(exx[:], dfx[:], AF.Exp, r=["dfx"], w=["exx"])
            P.tt("dve", scal[s2][:, 64:128], dtv[:], exx[:], ALU.mult, r=["dtv", "exx"], w=[f"scal{s2}"])
            P.dma("sp", dr["scal"][t], scal[s2][:], r=[f"scal{s2}"], sem=f"a1scal{s2}")
            P.tt("dve", csn[:, 64:128], lndt[:], csn[:, 0:64], ALU.subtract, r=["lndt", "csn"], w=["csn"])

        def ssd2b(t):
            s2 = t % 2
            bk = nb()
            P.tr(ps[:, bk, 0:128], csn[:], ident_f, r=["csn", "tri"], w=[f"ps{bk}"])
            P.copy("act", trh[s2][:], ps[:, bk, 0:128], r=[f"ps{bk}"], w=[f"trh{s2}"])
            P.tt("dve", trl[s2][:], ps[:, bk, 0:128], trh[s2][:], ALU.subtract, r=[f"ps{bk}", f"trh{s2}"], w=[f"trl{s2}"])
            P.dma("sp", dr["trd"][t, 0], trh[s2][:], r=[f"trh{s2}"], sem=f"a1trh{s2}")
            P.dma("sp", dr["trd"][t, 1], trl[s2][:], r=[f"trl{s2}"], sem=f"a1trl{s2}")

        front_a(0)
        front_b(0)
        for t in range(NTT):
            back(t)
        if "ssd" not in DBG.get("skip", ()):
            ssd2a(NTT - 1)
            ssd2b(NTT - 1)
        P.emit(f"a1_{L}")


SCRATCH = {
    "hT": ([D, S], BF16),
    "qT": ([8, 128, S], BF16),
    "kT": ([4, 128, S], BF16),
    "v": ([S, 256], BF16),
    "sz": ([S, DI], BF16),
    "scal": ([NT, 128, 128], F32),
    "trd": ([NT, 2, 128, 128], BF16),
    "cd": ([NT, 128, 64], F32),
    "xbcT": ([CONV, S], BF16),
    "mixT": ([3 * D, S], BF16),
    "xtm": ([S, DI], BF16),
    "btm": ([S, 512], BF16),
    "y1": ([S, DI], F32),
    "x1": ([S, D], F32),
    "x2": ([S, D], F32),
}
NPDT = {F32: np.float32, BF16: ml_dtypes.bfloat16}


def build(phases, dbg_out=(), dbg_in=()):
    nc = bass.Bass("TRN2", target_bir_lowering=False)
    C = Ctx(nc)
    C.dt("x", [S, D], F32, "ExternalInput")
    C.dt("w_in", [DEPTH, D, WIN], F32, "ExternalInput")
    C.dt("w_out", [DEPTH, 3 * D, D], F32, "ExternalInput")
    C.dt("w_up", [DEPTH, D, DFF], F32, "ExternalInput")
    C.dt("w_down", [DEPTH, DFF, D], F32, "ExternalInput")
    C.dt("consts", [128, const_layout()[1]], F32, "ExternalInput")
    C.dt("tri", [128, 512], F32, "ExternalInput")
    C.dt("bfc", [128, 384 + 4096], BF16, "ExternalInput")
    C.dt("rope_cos", [S, 64], F32, "ExternalInput")
    C.dt("rope_sin", [S, 64], F32, "ExternalInput")
    C.dt("out", [S, D], F32, "ExternalOutput")
    for name, (shape, dt) in SCRATCH.items():
        kind = "ExternalOutput" if name in dbg_out else ("ExternalInput" if name in dbg_in else None)
        C.dt(name, shape, dt, kind)
    for ph in phases:
        ph(C)
    return nc


def phase_a2(C, L):
    nc, P, dr = C.nc, C.P, C.dram
    coff, ncst = const_layout()
    with ExitStack() as es:
        def sb(name, shape, dt):
            return es.enter_context(nc.sbuf_tensor(f"a2{L}_{name}", shape, dt))

        ps = es.enter_context(nc.psum_tensor(f"a2{L}_ps", [128, 8, 512], F32))
        cst = sb("cst", [128, ncst], F32)
        hT = sb("hT", [128, 8, S], BF16)
        wx = sb("wx", [128, 8, CONV], BF16)
        raw = [sb(f"raw{i}", [128, S + 4], F32) for i in range(2)]
        segs = [(0, 2048, "dve"), (2048, 4096, "dve")]
        acc = [sb(f"acc{i}", [128, 2048], F32) for i in range(2)]
        outb = [sb(f"outb{i}", [128, S], BF16) for i in range(2)]

        P.dma("sp", cst[:], dr["consts"], w=["cst"], sem="a2cst")
        hsrc = dr["hT"].rearrange("(k p) t -> p k t", p=128)
        for i in range(8):
            P.dma("sp", hT[:, :, i * 512:(i + 1) * 512], hsrc[:, :, i * 512:(i + 1) * 512], w=[f"hT{i}"], sem=f"a2hT{i}")
        for cg in range(6):
            for k in range(8):
                P.dma("pool", wx[:, k, cg * 512:(cg + 1) * 512], dr["w_in"][L, k * 128:(k + 1) * 128, 3584 + cg * 512:3584 + (cg + 1) * 512],
                      w=[f"wx{cg}"], sem=f"a2wx{cg}")
        for i in range(2):
            P.memset("dve", raw[i][:, 0:2], 0.0, w=[f"rawpad{i}"])
            P.memset("dve", raw[i][:, S + 2:S + 4], 0.0, w=[f"rawpad{i}"])
        o_w, _ = coff[f"convw_{L}"]
        o_b, _ = coff[f"convb_{L}"]
        bkc = [0]
        NCC = DBG.get("a2_chunks", 24)

        def mmcopy(c):
            s2 = c % 2
            rs = raw[s2]
            for i in range(8):
                bk = bkc[0]
                for k in range(8):
                    P.mm(ps[:, bk, :], wx[:, k, c * 128:(c + 1) * 128], hT[:, k, i * 512:(i + 1) * 512],
                         start=(k == 0), stop=(k == 7), r=[f"wx{c // 4}", f"hT{i}"], w=[f"ps{bk}"])
                P.copy("act", rs[:, 2 + i * 512:2 + (i + 1) * 512], ps[:, bk, :], r=[f"ps{bk}"], w=[f"raw{s2}_{i}"])
                bkc[0] = (bk + 1) % 8

        def conv(c):
            s2 = c % 2
            rs = raw[s2]
            for si, (a, b, eng) in enumerate(segs):
                n = b - a
                rk = [f"raw{s2}_{i}" for i in range(max(0, (a - 2) // 512), min(7, (b + 1) // 512) + 1)] + [f"rawpad{s2}", "cst"]
                A = acc[si][:, 0:n]
                P.ts(eng, A, rs[:, a:a + n], cst[:, o_w + c * 5:o_w + c * 5 + 1], ALU.mult, r=rk, w=[f"acc{si}"])
                for j in range(1, 5):
                    wj = cst[:, o_w + c * 5 + j:o_w + c * 5 + j + 1]
                    P.stt(eng, A, rs[:, a + j:a + j + n], wj, A, ALU.mult, ALU.add, r=rk + [f"acc{si}"], w=[f"acc{si}"])
                P.act(outb[s2][:, a:b], A, AF.Silu, bias=cst[:, o_b + c:o_b + c + 1], r=[f"acc{si}", "cst"], w=[f"outb{s2}"])
            P.dma("sp", dr["xbcT"][c * 128:(c + 1) * 128, :], outb[s2][:], r=[f"outb{s2}"], sem=f"a2out{s2}")

        mmcopy(0)
        for c in range(NCC):
            if c + 1 < NCC:
                mmcopy(c + 1)
            conv(c)
        P.emit(f"a2_{L}")


def phase_attn(C, L):
    nc, P, dr = C.nc, C.P, C.dram
    with ExitStack() as es:
        def sb(name, shape, dt):
            return es.enter_context(nc.sbuf_tensor(f"at{L}_{name}", shape, dt))

        ps = es.enter_context(nc.psum_tensor(f"at{L}_ps", [128, 8, 512], F32))
        kT = sb("kT", [128, 4, S], BF16)
        vall = sb("vall", [128, NT, 4, 192], BF16)
        qT = [sb(f"qT{i}", [128, 8, 512], BF16) for i in range(2)]
        pT = [sb(f"pT{i}", [128, 2, 512], BF16) for i in range(3)]
        rcp = sb("rcp", [128, 512], F32)
        osb = [sb(f"osb{i}", [128, 512], F32) for i in range(4)]
        aT = [sb(f"aT{i}", [128, 8, 512], BF16) for i in range(2)]

        P.memset("pool", vall[:, :, :, 0:64], 1.0, w=["vall"])
        P.memset("pool", vall[:, :, :, 128:192], 1.0, w=["vall"])
        P.dma("sp", qT[0][:], dr["qT"].rearrange("k p t -> p k t")[:, :, 0:512], w=["qT0"], sem="atq0")
        P.dma("sp", kT[:, 0, :], dr["kT"][0], w=["kT0"], sem="atk0")
        vsrc = dr["v"].rearrange("(t p) (g d) -> p t g d", p=128, d=64)
        for g in range(4):
            P.dma("act", vall[:, :, g, 64:128], vsrc[:, :, g, :], r=[], w=["vall"], sem="atv")
        for g in range(1, 4):
            P.dma("sp", kT[:, g, :], dr["kT"][g], w=[f"kT{g}"], sem=f"atk{g}")
        groups = [3] * 10 + [2]
        NQC = DBG.get("at_qc", 8)
        def ld_q(qc):
            q2 = qc % 2
            P.dma("sp", qT[q2][:], dr["qT"].rearrange("k p t -> p k t")[:, :, qc * 512:(qc + 1) * 512], w=[f"qT{q2}"], sem=f"atq{q2}")

        for qc in range(NQC):
            q2 = qc % 2
            if qc + 1 < NQC:
                ld_q(qc + 1)
            for j in range(DBG.get("at_pairs", 8)):
                g = j // 2
                NU = NT // 2

                def s_mm(u):
                    for i in range(2):
                        kt = 2 * u + i
                        for hh in range(2):
                            st = (2 * u + hh) % 3
                            lo = 64 * hh
                            P.mm(ps[:, st * 2 + i, :], kT[lo:lo + 64, g, kt * 128:(kt + 1) * 128], qT[q2][lo:lo + 64, j, :],
                                 r=[f"kT{g}", f"qT{q2}"], w=[f"ps{st * 2 + i}"])

                def s_exp(u, hh):
                    st = (2 * u + hh) % 3
                    P.act(pT[st][:], ps[:, st * 2:st * 2 + 2, :], AF.Exp, scale=float(HD) ** -0.5,
                          r=[f"ps{st * 2}", f"ps{st * 2 + 1}"], w=[f"pT{st}"])

                def do_pv(u, hh):
                    st = (2 * u + hh) % 3
                    ob = 6 + hh
                    vlo = 64 if hh == 0 else 0
                    for i in range(2):
                        kt = 2 * u + i
                        P.mm(ps[:, ob, :], vall[:, kt, g, vlo:vlo + 128], pT[st][:, i, :],
                             start=(kt == 0), stop=(kt == NT - 1), r=["vall", f"pT{st}"], w=[f"ps{ob}"])

                for u in range(NU):
                    s_mm(u)
                    s_exp(u, 0)
                    if u >= 1:
                        do_pv(u - 1, 0)
                    s_exp(u, 1)
                    if u >= 1:
                        do_pv(u - 1, 1)
                do_pv(NU - 1, 0)
                do_pv(NU - 1, 1)
                for hh in range(2):
                    ob = 6 + hh
                    oi = (j % 2) * 2 + hh
                    P.copy("dve", osb[oi][:], ps[:, ob, :], r=[f"ps{ob}"], w=[f"osb{oi}"])
                for hh in range(2):
                    oi = (j % 2) * 2 + hh
                    olo, slo = (0, 64) if hh == 0 else (64, 0)
                    P.copy("dve", rcp[olo:olo + 64, :], osb[oi][slo:slo + 64, :], r=[f"osb{oi}"], w=[f"rcp{hh}"])
                    P.recip(rcp[olo:olo + 64, :], rcp[olo:olo + 64, :], r=[f"rcp{hh}"], w=[f"rcp{hh}"])
                    P.tt("dve", aT[q2][olo:olo + 64, j, :], osb[oi][olo:olo + 64, :], rcp[olo:olo + 64, :], ALU.mult,
                         r=[f"osb{oi}", f"rcp{hh}"], w=[f"aT{q2}"])
            P.dma("sp", dr["mixT"][0:D, :].rearrange("(j p) t -> p j t", p=128)[:, :, qc * 512:(qc + 1) * 512], aT[q2][:],
                  r=[f"aT{q2}"], sem=f"ataT{q2}")
        P.emit(f"attn_{L}")


def phase_sf(C, L):
    nc, P, dr = C.nc, C.P, C.dram
    coff, ncst = const_layout()
    with ExitStack() as es:
        def sb(name, shape, dt):
            return es.enter_context(nc.sbuf_tensor(f"sf{L}_{name}", shape, dt))

        ps = es.enter_context(nc.psum_tensor(f"sf{L}_ps", [128, 8, 512], F32))
        psT = ps[:, 4, :].bitcast(BF16)
        cst = sb("cst", [128, ncst], F32)
        bfc = sb("bfc", [128, 384], BF16)
        identb = bfc[:, 0:128]
        maskx = [sb(f"mask{d}", [128, 8, 128], BF16) for d in range(2)]
        diall = sb("diall", [128, 32, 128], BF16)
        xbc = [sb(f"xbc{i}", [128, 24, 256], BF16) for i in range(3)]
        l1 = [[sb(f"l1_{d}{i}", [128, 128], BF16) for i in range(3)] for d in range(2)]
        r1 = [[sb(f"r1_{d}{i}", [128, 4096], BF16) for i in range(3)] for d in range(2)]
        scal = [sb(f"scal{i}", [128, 128], F32) for i in range(3)]
        cdall = sb("cdall", [128, NT, 64], F32)
        E = [[sb(f"E{d}{i}", [128, 1024], BF16) for i in range(2)] for d in range(2)]
        T = sb("T", [128, 1024], BF16)
        M = [sb(f"M{i}", [128, 1024], BF16) for i in range(2)]
        cb = [sb(f"cb{i}", [128, 128], BF16) for i in range(2)]
        xtm = [sb(f"xtm{i}", [128, 2048], BF16) for i in range(2)]
        btm = [sb(f"btm{i}", [128, 512], BF16) for i in range(2)]
        xw = sb("xw", [128, 512], BF16)
        prev = sb("prev", [128, 2048], F32)
        prevb = sb("prevb", [128, 2048], BF16)
        ytmp = sb("ytmp", [128, 512], F32)
        y1 = [sb(f"y1{i}", [128, 2048], F32) for i in range(2)]

        P.dma("sp", cst[:], dr["consts"], w=["cst"], sem="sfcst")
        P.dma("sp", bfc[:], dr["bfc"][:, 0:384], w=["bfc"], sem="sfbfc")
        P.dma("sp", cdall[:], dr["cd"].rearrange("c p h -> p c h"), w=["cdall"], sem="sfcd")
        for d in range(2):
            P.copy("dve", maskx[d][:], bfc[:, 128 + d * 128:256 + d * 128].unsqueeze(1).to_broadcast([128, 8, 128]),
                   r=["bfc"], w=[f"mask{d}"])
            for i in range(3):
                P.memset("dve", l1[d][i][:], 0.0, w=[f"l1_{d}{i}"])
                P.memset("dve", r1[d][i][:], 0.0, w=[f"r1_{d}{i}"])
                P.dma("sp", l1[d][i][0:2, :], dr["bfc"][64:66, 384:512], w=[f"l1_{d}{i}"], sem=f"sfl1{d}{i}")
                P.dma("sp", r1[d][i][2:66, :], dr["bfc"][0:64, 384:4480], w=[f"r1_{d}{i}"], sem=f"sfr1{d}{i}")
        o_d, _ = coff[f"dskip_{L}"]
        P.tt("dve", diall[:], identb.unsqueeze(1).to_broadcast([128, 32, 128]),
             cst[:, o_d:o_d + 32].unsqueeze(2).to_broadcast([128, 32, 128]), ALU.mult, r=["bfc", "cst"], w=["diall"])
        P.memset("dve", prev[:], 0.0, w=[f"prev{g}" for g in range(NG)])
        P.memset("dve", prevb[:], 0.0, w=[f"prevb{g}" for g in range(NG)])

        xsrc = dr["xbcT"].rearrange("(k p) t -> p k t", p=128)
        NCH = DBG.get("ssd_chunks", NT)

        def loads(c):
            s3 = c % 3
            sc = (c // 2) % 3
            if c % 2 == 0:
                P.dma("sp", xbc[sc][:], xsrc[:, :, c * 128:c * 128 + 256], w=[f"xbc{sc}"], sem=f"sfxbc{sc}")
            P.dma("sp", scal[s3][:], dr["scal"][c], w=[f"scal{s3}"], sem=f"sfscal{s3}")
            for d in range(2):
                P.dma("sp", l1[d][s3][2:34, :], dr["trd"][c, 0, 64 + d * 32:96 + d * 32, :], w=[f"l1_{d}{s3}"], sem=f"sfl1{d}{s3}")
                P.dma("sp", l1[d][s3][34:66, :], dr["trd"][c, 1, 64 + d * 32:96 + d * 32, :], w=[f"l1_{d}{s3}"], sem=f"sfl1{d}{s3}")
                for hl in range(2):
                    P.dma("sp", r1[d][s3][hl:hl + 1, :],
                          dr["trd"][c, hl:hl + 1, d * 32:d * 32 + 32, :].rearrange("a h l -> a (h l)"),
                          w=[f"r1_{d}{s3}"], sem=f"sfr1{d}{s3}")

        def stage_a(c, g, sl):
            s2 = c % 2
            s3 = c % 3
            sc = (c // 2) % 3
            cc = c % 2
            if g == 0:
                if c == 0:
                    loads(0)
                if c + 1 < NCH:
                    loads(c + 1)
            tk = slice(cc * 128, (cc + 1) * 128)
            for i in range(4):
                P.tr(psT[:, i * 128:(i + 1) * 128], xbc[sc][:, 4 * g + i, tk], identb, r=[f"xbc{sc}", "bfc"], w=["ps4"])
            P.tr(psT[:, 512:640], xbc[sc][:, 16 + g, tk], identb, r=[f"xbc{sc}", "bfc"], w=["ps4"])
            P.mm(ps[:, 4, 320:448], xbc[sc][:, 16 + g, tk], xbc[sc][:, 20 + g, tk], r=[f"xbc{sc}"], w=["ps4"])
            P.copy("act", xtm[s2][:, g * 512:(g + 1) * 512], psT[:, 0:512], r=["ps4"], w=[f"xtm{s2}_{g}"])
            P.copy("act", btm[s2][:, g * 128:(g + 1) * 128], psT[:, 512:640], r=["ps4"], w=[f"btm{s2}_{g}"])
            P.copy("act", cb[sl][:], ps[:, 4, 320:448], r=["ps4"], w=[f"cb{sl}"])
            for d in range(2):
                b0 = 2 * d
                for hb in range(2):
                    cols = slice((g * 8 + hb * 4) * 128, (g * 8 + hb * 4 + 4) * 128)
                    P.mm(ps[:, b0 + hb, :], l1[d][s3][:], r1[d][s3][:, cols], start=True, stop=False,
                         r=[f"l1_{d}{s3}", f"r1_{d}{s3}"], w=[f"ps{b0 + hb}"])
                    P.mm(ps[:, b0 + hb, :], identb, maskx[d][:, hb * 4:hb * 4 + 4, :], start=False, stop=True,
                         r=["bfc", f"mask{d}"], w=[f"ps{b0 + hb}"])
                P.act(E[d][sl][:].rearrange("p (a b) -> p a b", a=2), ps[:, b0:b0 + 2, :], AF.Exp,
                      r=[f"ps{b0}", f"ps{b0 + 1}"], w=[f"E{d}{sl}"])

        def hdr(c):
            return c % 2, (c // 2) % 3, c % 3, slice((c % 2) * 128, (c % 2 + 1) * 128)

        def s2d(c, g, sl):
            s2, sc, s3, tk = hdr(c)
            P.tt("dve", T[:], E[0][sl][:], E[1][sl][:], ALU.add, r=[f"E0{sl}", f"E1{sl}"], w=["T"])
            P.tt("dve", M[sl][:].rearrange("p (h l) -> p h l", l=128), T[:].rearrange("p (h l) -> p h l", l=128),
                 cb[sl][:].unsqueeze(1).to_broadcast([128, 8, 128]), ALU.mult, r=["T", f"cb{sl}"], w=[f"M{sl}"])
            wf = scal[s3][:, 64 + g * 8:64 + g * 8 + 8]
            P.tt("dve", xw[:].rearrange("p (e d) -> p e d", d=64), xtm[s2][:, g * 512:(g + 1) * 512].rearrange("p (e d) -> p e d", d=64),
                 wf.unsqueeze(2).to_broadcast([128, 8, 64]), ALU.mult, r=[f"xtm{s2}_{g}", f"scal{s3}"], w=["xw"])

        def s2p(c, g, sl):
            s2, sc, s3, tk = hdr(c)
            for e in range(8):
                h = g * 8 + e
                xh = xtm[s2][:, g * 512 + e * 64:g * 512 + (e + 1) * 64]
                P.mm(ps[:, 6, e * 64:(e + 1) * 64], M[sl][:, e * 128:(e + 1) * 128], xh, start=True, stop=False,
                     r=[f"M{sl}", f"xtm{s2}_{g}"], w=["ps6"])
                P.mm(ps[:, 6, e * 64:(e + 1) * 64], diall[:, h, :], xh, start=False, stop=True,
                     r=["diall", f"xtm{s2}_{g}"], w=["ps6"])
            P.mm(ps[:, 7, :], xbc[sc][:, 20 + g, tk], prevb[:, g * 512:(g + 1) * 512], r=[f"xbc{sc}", f"prevb{g}"], w=["ps7"])
            P.mm(ps[:, 5, :], btm[s2][:, g * 128:(g + 1) * 128], xw[:], r=[f"btm{s2}_{g}", "xw"], w=["ps5"])

        def s3_(c, g, sl):
            s2, sc, s3, tk = hdr(c)
            ef = scal[s3][:, g * 8:g * 8 + 8]
            P.tt("dve", ytmp[:].rearrange("p (e d) -> p e d", d=64), ps[:, 7, :].rearrange("p (e d) -> p e d", d=64),
                 ef.unsqueeze(2).to_broadcast([128, 8, 64]), ALU.mult, r=["ps7", f"scal{s3}"], w=["ytmp"])
            P.tt("dve", y1[s2][:, g * 512:(g + 1) * 512], ytmp[:], ps[:, 6, :], ALU.add, r=["ytmp", "ps6"], w=[f"y1{s2}"])
            pg = prev[:, g * 512:(g + 1) * 512]
            cdf = cdall[:, c, g * 8:g * 8 + 8]
            P.tt("dve", pg.rearrange("p (e d) -> p e d", d=64), pg.rearrange("p (e d) -> p e d", d=64),
                 cdf.unsqueeze(2).to_broadcast([128, 8, 64]), ALU.mult, r=[f"prev{g}", "cdall"], w=[f"prev{g}"])
            P.tt("dve", pg, pg, ps[:, 5, :], ALU.add, r=[f"prev{g}", "ps5"], w=[f"prev{g}"])
            P.copy("act", prevb[:, g * 512:(g + 1) * 512], pg, r=[f"prev{g}"], w=[f"prevb{g}"])
            if g == NG - 1:
                rows = slice(c * 128, (c + 1) * 128)
                P.dma("sp", dr["xtm"][rows, :], xtm[s2][:], r=[f"xtm{s2}_{gg}" for gg in range(4)], sem=f"sfxtm{s2}")
                P.dma("sp", dr["btm"][rows, :], btm[s2][:], r=[f"btm{s2}_{gg}" for gg in range(4)], sem=f"sfbtm{s2}")
                P.dma("sp", dr["y1"][rows, :], y1[s2][:], r=[f"y1{s2}"], sem=f"sfy1{s2}")

        items = [(c, g) for c in range(NCH) for g in range(NG)]
        n_it = len(items)
        stage_a(*items[0], 0)
        if n_it > 1:
            stage_a(*items[1], 1)
        s2d(*items[0], 0)
        s2p(*items[0], 0)
        for k in range(n_it):
            if k + 2 < n_it:
                stage_a(*items[k + 2], (k + 2) % 2)
            if k + 1 < n_it:
                s2d(*items[k + 1], (k + 1) % 2)
            s3_(*items[k], k % 2)
            if k + 1 < n_it:
                s2p(*items[k + 1], (k + 1) % 2)
        P.emit(f"sf_{L}")


def phase_sb(C, L):
    nc, P, dr = C.nc, C.P, C.dram
    coff, ncst = const_layout()
    with ExitStack() as es:
        def sb(name, shape, dt):
            return es.enter_context(nc.sbuf_tensor(f"sb{L}_{name}", shape, dt))

        ps = es.enter_context(nc.psum_tensor(f"sb{L}_ps", [128, 8, 512], F32))
        cst = sb("cst", [128, ncst], F32)
        identb = sb("identb", [128, 128], BF16)
        cdall = sb("cdall", [128, NT, 64], F32)
        epsb = sb("epsb", [128, 1], F32)
        ct = [sb(f"ct{i}", [128, 4, 128], BF16) for i in range(3)]
        xtm = [sb(f"xtm{i}", [128, 2048], BF16) for i in range(3)]
        btm = [sb(f"btm{i}", [128, 512], BF16) for i in range(3)]
        y1 = [sb(f"y1{i}", [128, 2048], F32) for i in range(3)]
        sz = [sb(f"sz{i}", [128, 2048], BF16) for i in range(3)]
        scal = [sb(f"scal{i}", [128, 128], F32) for i in range(3)]
        xw = sb("xw", [128, 512], BF16)
        prev = sb("prev", [128, 2048], F32)
        prevb = sb("prevb", [128, 2048], BF16)
        ytmp = sb("ytmp", [128, 512], F32)
        yy = [sb(f"yy{i}", [128, 512], F32) for i in range(2)]
        gg = [sb(f"gg{i}", [128, 512], F32) for i in range(2)]
        junk = sb("junk", [128, 512], BF16)
        ssq = sb("ssq", [128, 4], F32)
        gn = [sb(f"gn{i}", [128, 512], BF16) for i in range(2)]
        ynT = [sb(f"ynT{i}", [128, 16, 512], BF16) for i in range(2)]

        P.dma("sp", cst[:], dr["consts"], w=["cst"], sem="sbcst")
        P.dma("sp", identb[:], dr["bfc"][:, 0:128], w=["identb"], sem="sbid")
        P.dma("sp", cdall[:], dr["cd"].rearrange("c p h -> p c h"), w=["cdall"], sem="sbcd")
        P.memset("dve", epsb[:], NORM_EPS, w=["epsb"])
        P.memset("dve", prev[:], 0.0, w=[f"prev{g}" for g in range(NG)])
        P.memset("dve", prevb[:], 0.0, w=[f"prevb{g}" for g in range(NG)])
        o_w, _ = coff[f"ssmw_{L}"]
        csrc = dr["xbcT"][2560:3072, :].rearrange("(g p) t -> p g t", p=128)
        NCH = DBG.get("ssd_chunks", NT)
        bank = [0]
        trb = {}

        def loads(ci):
            c = NCH - 1 - ci
            s2 = ci % 3
            rows = slice(c * 128, (c + 1) * 128)
            P.dma("sp", ct[s2][:], csrc[:, :, rows], w=[f"ct{s2}"], sem=f"sbct{s2}")
            P.dma("sp", xtm[s2][:], dr["xtm"][rows, :], w=[f"xtm{s2}"], sem=f"sbxtm{s2}")
            P.dma("sp", btm[s2][:], dr["btm"][rows, :], w=[f"btm{s2}"], sem=f"sbbtm{s2}")
            P.dma("sp", y1[s2][:], dr["y1"][rows, :], w=[f"y1{s2}"], sem=f"sby1{s2}")
            P.dma("sp", sz[s2][:], dr["sz"][rows, :], w=[f"sz{s2}"], sem=f"sbsz{s2}")
            P.dma("sp", scal[s2][:], dr["scal"][c], w=[f"scal{s2}"], sem=f"sbscal{s2}")

        def stage_a(ci, g, sl):
            c = NCH - 1 - ci
            s2 = ci % 3
            if g == 0:
                if ci == 0:
                    loads(0)
                if ci + 1 < NCH:
                    loads(ci + 1)
            gs = slice(g * 512, (g + 1) * 512)
            b_off, b_st = bank[0] % 8, (bank[0] + 1) % 8
            bank[0] += 2
            P.mm(ps[:, b_off, :], ct[s2][:, g, :], prevb[:, gs], r=[f"ct{s2}", f"prevb{g}"], w=[f"ps{b_off}"])
            wb = scal[s2][:, 96 + g * 8:96 + g * 8 + 8]
            P.tt("dve", xw[:].rearrange("p (e d) -> p e d", d=64), xtm[s2][:, gs].rearrange("p (e d) -> p e d", d=64),
                 wb.unsqueeze(2).to_broadcast([128, 8, 64]), ALU.mult, r=[f"xtm{s2}", f"scal{s2}"], w=["xw"])
            P.mm(ps[:, b_st, :], btm[s2][:, g * 128:(g + 1) * 128], xw[:], r=[f"btm{s2}", "xw"], w=[f"ps{b_st}"])
            eb = scal[s2][:, 32 + g * 8:32 + g * 8 + 8]
            P.tt("dve", ytmp[:].rearrange("p (e d) -> p e d", d=64), ps[:, b_off, :].rearrange("p (e d) -> p e d", d=64),
                 eb.unsqueeze(2).to_broadcast([128, 8, 64]), ALU.mult, r=[f"ps{b_off}", f"scal{s2}"], w=["ytmp"])
            P.tt("dve", yy[sl][:], ytmp[:], y1[s2][:, gs], ALU.add, r=["ytmp", f"y1{s2}"], w=[f"yy{sl}"])
            pg = prev[:, gs]
            cdb = cdall[:, c, 32 + g * 8:32 + g * 8 + 8]
            P.tt("dve", pg.rearrange("p (e d) -> p e d", d=64), pg.rearrange("p (e d) -> p e d", d=64),
                 cdb.unsqueeze(2).to_broadcast([128, 8, 64]), ALU.mult, r=[f"prev{g}", "cdall"], w=[f"prev{g}"])
            P.tt("dve", pg, pg, ps[:, b_st, :], ALU.add, r=[f"prev{g}", f"ps{b_st}"], w=[f"prev{g}"])
            P.copy("act", prevb[:, gs], pg, r=[f"prev{g}"], w=[f"prevb{g}"])

        def stage_b(ci, g, sl):
            c = NCH - 1 - ci
            s2 = ci % 3
            gs = slice(g * 512, (g + 1) * 512)
            P.tt("dve", gg[sl][:], yy[sl][:], sz[s2][:, gs], ALU.mult, r=[f"yy{sl}", f"sz{s2}"], w=[f"gg{sl}"])
            P.act(junk[:], gg[sl][:], AF.Square, accum_out=ssq[:, 0:1], r=[f"gg{sl}"], w=["junk", "ssq0"])
            P.act(ssq[:, 1:2], ssq[:, 0:1], AF.Ln, scale=1.0 / 512, bias=epsb[:, 0:1], r=["ssq0", "epsb"], w=["ssq1"])
            P.act(ssq[:, 2:3], ssq[:, 1:2], AF.Exp, scale=-0.5, r=["ssq1"], w=["ssq2"])
            P.act(gn[sl][:], gg[sl][:], AF.Copy, scale=ssq[:, 2:3], r=[f"gg{sl}", "ssq2"], w=[f"gn{sl}"])

        def stage_b1b(ci, g, sl):
            b_tr = bank[0] % 8
            bank[0] += 1
            psT = ps[:, b_tr, :].bitcast(BF16)
            for i in range(4):
                P.tr(psT[:, i * 128:(i + 1) * 128], gn[sl][:, i * 128:(i + 1) * 128], identb[:], r=[f"gn{sl}", "identb"], w=[f"ps{b_tr}"])
            trb[(ci, g)] = b_tr

        def stage_b2(ci, g, sl):
            c = NCH - 1 - ci
            q4 = c % 4
            oc = c // 4
            o2 = oc % 2
            b_tr = trb.pop((ci, g))
            psT = ps[:, b_tr, :].bitcast(BF16)
            P.copy("act", ynT[o2][:, 4 * g:4 * g + 4, q4 * 128:(q4 + 1) * 128], psT[:, 0:512].rearrange("p (k t) -> p k t", k=4),
                   r=[f"ps{b_tr}"], w=[f"ynT{o2}"])
            if q4 == 0 and g == NG - 1:
                P.dma("sp", dr["mixT"][D:3 * D, :].rearrange("(k p) t -> p k t", p=128)[:, :, oc * 512:(oc + 1) * 512], ynT[o2][:],
                      r=[f"ynT{o2}"], sem=f"sbyn{o2}")

        items = [(ci, g) for ci in range(NCH) for g in range(NG)]
        stage_a(*items[0], 0)
        for i, it in enumerate(items):
            if i + 1 < len(items):
                stage_a(*items[i + 1], (i + 1) % 2)
            stage_b(*it, i % 2)
            if i >= 1:
                stage_b1b(*items[i - 1], (i - 1) % 2)
            if i >= 2:
                stage_b2(*items[i - 2], (i - 2) % 2)
        n_it = len(items)
        stage_b1b(*items[-1], (n_it - 1) % 2)
        if n_it >= 2:
            stage_b2(*items[-2], (n_it - 2) % 2)
        stage_b2(*items[-1], (n_it - 1) % 2)
        P.emit(f"sb_{L}")


def phase_d1(C, L, x_src):
    nc, P, dr = C.nc, C.P, C.dram
    coff, ncst = const_layout()
    with ExitStack() as es:
        def sb(name, shape, dt):
            return es.enter_context(nc.sbuf_tensor(f"d1{L}_{name}", shape, dt))

        ps = es.enter_context(nc.psum_tensor(f"d1{L}_ps", [128, 6, 512], F32))
        psb = es.enter_context(nc.psum_tensor(f"d1{L}_psb", [128, 2, 1024], BF16))
        cst = sb("cst", [128, ncst], F32)
        identb = sb("identb", [128, 128], BF16)
        epsb = sb("epsb", [128, 1], F32)
        wo = sb("wo", [128, 24, D], BF16)
        mix = [sb(f"mix{i}", [128, 24, 512], BF16) for i in range(2)]
        xs = [sb(f"x{i}", [128, D], F32) for i in range(3)]
        x1 = [sb(f"x1{i}", [128, D], F32) for i in range(2)]
        junk = sb("junk", [128, D], BF16)
        ss = [sb(f"ss{i}", [128, 4], F32) for i in range(2)]
        hbf = [sb(f"h{i}", [128, D], BF16) for i in range(2)]
        hT = [sb(f"hT{i}", [128, 8, 512], BF16) for i in range(2)]

        P.dma("sp", cst[:], dr["consts"], w=["cst"], sem="d1cst")
        P.dma("sp", identb[:], dr["bfc"][:, 0:128], w=["identb"], sem="d1id")
        P.memset("dve", epsb[:], NORM_EPS, w=["epsb"])
        for hf in range(2):
            for k in range(24):
                P.dma("pool", wo[:, k, hf * 512:(hf + 1) * 512], dr["w_out"][L, k * 128:(k + 1) * 128, hf * 512:(hf + 1) * 512],
                      w=[f"wo{hf}_{k // 8}"], sem=f"d1wo{hf}_{k // 8}")
        o_l, _ = coff[f"ln2_{L}"]
        o_sw, _ = coff[f"ssmw_{L}"]
        for hf in range(2):
            for k in range(8, 24):
                P.ts("dve", wo[:, k, hf * 512:(hf + 1) * 512], wo[:, k, hf * 512:(hf + 1) * 512], cst[:, o_sw + k - 8:o_sw + k - 7], ALU.mult,
                     r=[f"wo{hf}_{k // 8}", "cst"], w=[f"wo{hf}_{k // 8}"])
        msrc = dr["mixT"].rearrange("(k p) t -> p k t", p=128)
        fb = 0
        NTT = DBG.get("d_tiles", NT)

        def ld_mix(ch):
            c2 = ch % 2
            for k3 in range(3):
                P.dma("sp", mix[c2][:, k3 * 8:(k3 + 1) * 8, :], msrc[:, k3 * 8:(k3 + 1) * 8, ch * 512:(ch + 1) * 512],
                      w=[f"mix{c2}_{k3}"], sem=f"d1mix{c2}_{k3}")

        def ld_x(t):
            x3 = t % 3
            P.dma("sp", xs[x3][:], x_src[t * 128:(t + 1) * 128, :], w=[f"x{x3}"], sem=f"d1x{x3}")

        bbk = [0]

        def trans(t):
            s2 = t % 2
            ch = t // 4
            c2 = ch % 2
            j = t % 4
            b = bbk[0]
            bbk[0] = (b + 1) % 2
            for k in range(8):
                P.tr(psb[:, b, k * 128:(k + 1) * 128], hbf[s2][:, k * 128:(k + 1) * 128], identb[:], r=[f"h{s2}", "identb"], w=[f"psb{b}"])
            P.tt("dve", hT[c2][:, :, j * 128:(j + 1) * 128], psb[:, b, :].rearrange("p (k t) -> p k t", k=8),
                 cst[:, o_l:o_l + 8].unsqueeze(2).to_broadcast([128, 8, 128]), ALU.mult, r=[f"psb{b}", "cst"], w=[f"hT{c2}_{j}"])
            if j == 3:
                P.dma("sp", dr["hT"].rearrange("(k p) t -> p k t", p=128)[:, :, ch * 512:(ch + 1) * 512], hT[c2][:],
                      r=[f"hT{c2}_{jj}" for jj in range(4)], sem=f"d1hT{c2}")

        for t in range(NTT):
            s2 = t % 2
            ch = t // 4
            c2 = ch % 2
            j = t % 4
            if t == 0:
                ld_mix(0)
                ld_x(0)
                ld_x(1)
            if j == 0 and ch + 1 < NTT // 4:
                ld_mix(ch + 1)
            if t + 2 < NTT:
                ld_x(t + 2)
            x3 = t % 3
            for hf in range(2):
                bk = fb
                fb = (fb + 1) % 6
                for k in range(24):
                    P.mm(ps[:, bk, :], mix[c2][:, k, j * 128:(j + 1) * 128], wo[:, k, hf * 512:(hf + 1) * 512],
                         start=(k == 0), stop=(k == 23), r=[f"mix{c2}_{k // 8}", f"wo{hf}_{k // 8}"], w=[f"ps{bk}"])
                P.tt("dve", x1[s2][:, hf * 512:(hf + 1) * 512], ps[:, bk, :], xs[x3][:, hf * 512:(hf + 1) * 512], ALU.add,
                     r=[f"ps{bk}", f"x{x3}"], w=[f"x1{s2}_{hf}"])
            P.dma("sp", dr["x1"][t * 128:(t + 1) * 128, :], x1[s2][:], r=[f"x1{s2}_0", f"x1{s2}_1"], sem=f"d1x1{s2}")
            xk = [f"x1{s2}_0", f"x1{s2}_1"]
            P.act(junk[:], x1[s2][:], AF.Square, accum_out=ss[s2][:, 0:1], r=xk, w=["junk", f"ss{s2}"])
            P.act(ss[s2][:, 1:2], ss[s2][:, 0:1], AF.Ln, scale=1.0 / D, bias=epsb[:, 0:1], r=[f"ss{s2}", "epsb"], w=[f"ssb{s2}"])
            P.act(ss[s2][:, 2:3], ss[s2][:, 1:2], AF.Exp, scale=-0.5, r=[f"ssb{s2}"], w=[f"ssc{s2}"])
            P.act(hbf[s2][:], x1[s2][:], AF.Copy, scale=ss[s2][:, 2:3], r=xk + [f"ssc{s2}"], w=[f"h{s2}"])
            if t >= 1:
                trans(t - 1)
        trans(NTT - 1)
        P.emit(f"d1_{L}")


def phase_d2(C, L, last):
    nc, P, dr = C.nc, C.P, C.dram
    coff, ncst = const_layout()
    with ExitStack() as es:
        def sb(name, shape, dt):
            return es.enter_context(nc.sbuf_tensor(f"d2{L}_{name}", shape, dt))

        ps = es.enter_context(nc.psum_tensor(f"d2{L}_ps", [128, 8, 512], F32))
        cst = sb("cst", [128, ncst], F32)
        epsb = sb("epsb", [128, 1], F32)
        wu = sb("wu", [128, 8, DFF], BF16)
        wd = sb("wd", [128, 32, D], BF16)
        hT = [sb(f"hT{i}", [128, 8, 512], BF16) for i in range(2)]
        uT = sb("uT", [128, 32, 512], BF16)
        rl = [sb(f"rl{i}", [128, 512], F32) for i in range(2)]
        x1 = [sb(f"x1{i}", [128, D], F32) for i in range(3)]
        x2 = [sb(f"x2{i}", [128, D], F32) for i in range(2)]
        ss = [sb(f"ss{i}", [128, 4], F32) for i in range(2)]
        P.dma("sp", cst[:], dr["consts"], w=["cst"], sem="d2cst")
        P.memset("dve", epsb[:], NORM_EPS, w=["epsb"])
        for fg in range(8):
            for k in range(8):
                P.dma("pool", wu[:, k, fg * 512:(fg + 1) * 512], dr["w_up"][L, k * 128:(k + 1) * 128, fg * 512:(fg + 1) * 512],
                      w=[f"wu{fg}"], sem=f"d2wu{fg}")
        for f in range(32):
            P.dma("pool", wd[:, f, :], dr["w_down"][L, f * 128:(f + 1) * 128, :], w=[f"wd{f // 4}"], sem=f"d2wd{f // 4}")
        o_f, _ = coff["fnw"]
        hsrc = dr["hT"].rearrange("(k p) t -> p k t", p=128)
        bk = 0
        NCHK = DBG.get("d_tiles", NT) // 4

        def ld_h(ch):
            c2 = ch % 2
            P.dma("sp", hT[c2][:], hsrc[:, :, ch * 512:(ch + 1) * 512], w=[f"hT{c2}"], sem=f"d2hT{c2}")

        def ld_x1(t):
            x3 = t % 3
            P.dma("sp", x1[x3][:], dr["x1"][t * 128:(t + 1) * 128, :], w=[f"x1{x3}"], sem=f"d2x1{x3}")

        ld_h(0)
        ld_x1(0)
        ld_x1(1)
        for ch in range(NCHK):
            c2 = ch % 2
            if ch + 1 < NCHK:
                ld_h(ch + 1)
            for f in range(32):
                for k in range(8):
                    P.mm(ps[:, bk, :], wu[:, k, f * 128:(f + 1) * 128], hT[c2][:, k, :], start=(k == 0), stop=(k == 7),
                         r=[f"wu{f // 4}", f"hT{c2}"], w=[f"ps{bk}"])
                r2 = f % 2
                P.act(rl[r2][:], ps[:, bk, :], AF.Relu, r=[f"ps{bk}"], w=[f"rl{r2}"])
                P.tt("dve", uT[:, f, :], rl[r2][:], rl[r2][:], ALU.mult, r=[f"rl{r2}"], w=[f"uT{f}"])
                bk = (bk + 1) % 8
            for j in range(4):
                t = ch * 4 + j
                s2 = t % 2
                x3 = t % 3
                if t + 2 < NCHK * 4:
                    ld_x1(t + 2)
                for hf in range(2):
                    for f in range(32):
                        P.mm(ps[:, bk, :], uT[:, f, j * 128:(j + 1) * 128], wd[:, f, hf * 512:(hf + 1) * 512],
                             start=(f == 0), stop=(f == 31), r=[f"uT{f}", f"wd{f // 4}"], w=[f"ps{bk}"])
                    P.tt("dve", x2[s2][:, hf * 512:(hf + 1) * 512], ps[:, bk, :], x1[x3][:, hf * 512:(hf + 1) * 512], ALU.add,
                         r=[f"ps{bk}", f"x1{x3}"], w=[f"x2{s2}_{hf}"])
                    bk = (bk + 1) % 8
                xk = [f"x2{s2}_0", f"x2{s2}_1"]
                if not last:
                    P.dma("sp", dr["x2"][t * 128:(t + 1) * 128, :], x2[s2][:], r=xk, sem=f"d2x2{s2}")
                else:
                    P.act(rl[0][:].bitcast(BF16), x2[s2][:], AF.Square, accum_out=ss[s2][:, 0:1], r=xk, w=["rl0", f"ss{s2}"])
                    P.act(ss[s2][:, 1:2], ss[s2][:, 0:1], AF.Ln, scale=1.0 / D, bias=epsb[:, 0:1], r=[f"ss{s2}", "epsb"], w=[f"ssb{s2}"])
                    P.act(ss[s2][:, 2:3], ss[s2][:, 1:2], AF.Exp, scale=-0.5, r=[f"ssb{s2}"], w=[f"ssc{s2}"])
                    P.stt("dve", x2[s2][:], x2[s2][:], ss[s2][:, 2:3], cst[:, o_f:o_f + D], ALU.mult, ALU.mult,
                          r=xk + [f"ssc{s2}", "cst"], w=xk)
                    P.dma("sp", dr["out"][t * 128:(t + 1) * 128, :], x2[s2][:], r=xk, sem=f"d2ob{s2}")
        P.emit(f"d2_{L}")


def all_phases():
    phs = []
    for L in range(DEPTH):
        xin = "x" if L == 0 else "x2"
        phs.append(lambda C, L=L, xin=xin: phase_a1(C, L, C.dram[xin]))
        phs.append(lambda C, L=L: phase_a2(C, L))
        phs.append(lambda C, L=L: phase_attn(C, L))
        phs.append(lambda C, L=L: phase_sf(C, L))
        phs.append(lambda C, L=L: phase_sb(C, L))
        phs.append(lambda C, L=L, xin=xin: phase_d1(C, L, C.dram[xin]))
        phs.append(lambda C, L=L: phase_d2(C, L, L == DEPTH - 1))
    return phs


_NC_CACHE = {}


def kernel(**inputs):
    inp = {k: np.asarray(v) for k, v in inputs.items()}
    if "nc" not in _NC_CACHE:
        _NC_CACHE["nc"] = build(all_phases())
    nc = _NC_CACHE["nc"]
    cos, sin = rope_tables()
    tri, bfc = fixed_consts()
    consts = pack_consts(inp)
    shared = {
        "w_in": np.ascontiguousarray(inp["w_in"], np.float32), "w_out": np.ascontiguousarray(inp["w_out"], np.float32),
        "w_up": np.ascontiguousarray(inp["w_up"], np.float32), "w_down": np.ascontiguousarray(inp["w_down"], np.float32),
        "consts": consts, "tri": tri, "bfc": bfc, "rope_cos": cos, "rope_sin": sin,
    }
    x = np.ascontiguousarray(inp["x"], np.float32)
    in_maps = [dict(shared, x=x[b]) for b in range(8)]
    res = run_bass_kernel_spmd(nc, in_maps, core_ids=list(range(8)))
    return np.stack([np.asarray(res.results[b]["out"], np.float32) for b in range(8)], axis=0)
```

```python
import numpy as np
from contextlib import ExitStack
import ml_dtypes
import concourse.bass as bass
import concourse.mybir as mybir
from concourse.bass_utils import run_bass_kernel_spmd

F32 = mybir.dt.float32
BF16 = mybir.dt.bfloat16
AF = mybir.ActivationFunctionType
ALU = mybir.AluOpType
AX = mybir.AxisListType

D = 1024
S = 4096
NT = S // 128
DEPTH = 2
NQ, NKV, HD = 16, 4, 64
DI = 2048
NH = 32
NG = 4
DS = 128
CONV = 3072
WIN = 6720
DFF = 4096
NORM_EPS = 1e-5
QK_EPS = 1e-6
BIG = 30000.0

ENGS = ("pe", "act", "dve", "pool", "sp")
DBG = {}


class DmaSem:
    def __init__(self, nc, name):
        self.h = nc.alloc_semaphore(name)
        self.n = 0


class Op:
    __slots__ = ("eng", "fn", "deps", "cdeps", "marked", "is_dma", "sem", "val", "idx")

    def __init__(self, eng, fn, is_dma):
        self.eng = eng
        self.fn = fn
        self.deps = []
        self.cdeps = {}
        self.marked = False
        self.is_dma = is_dma
        self.sem = None
        self.val = 0
        self.idx = 0


class Planner:
    def __init__(self, nc):
        self.nc = nc
        self.eng_sem = {e: nc.alloc_semaphore("es_" + e) for e in ("pe", "act", "dve", "pool")}
        self.eng_cnt = {e: 0 for e in ("pe", "act", "dve", "pool")}
        self.dma_sems = {}
        self.free_dsems = {"sw": [], "hw": []}
        self.n_dsems = 0
        self.reset()

    def reset(self):
        self.ops = {e: [] for e in ENGS}
        self.last_w = {}
        self.readers = {}
        self.nops = 0

    def dsem(self, name, q="sp"):
        kind = "sw" if q == "pool" else "hw"
        if name not in self.dma_sems:
            fl = self.free_dsems[kind]
            if fl:
                self.dma_sems[name] = fl.pop()
            else:
                self.dma_sems[name] = DmaSem(self.nc, f"ds{kind}{self.n_dsems}")
                self.dma_sems[name].kind = kind
                self.n_dsems += 1
        assert self.dma_sems[name].kind == kind, name
        return self.dma_sems[name]

    def _add_dep(self, op, d, raw):
        if d is op:
            return
        if d.eng == op.eng and not d.is_dma and not op.is_dma:
            if op.eng == "pe":
                return
            if not raw and not (self.ops[op.eng] and self.ops[op.eng][-1] is d):
                return
        if d.is_dma and op.is_dma and d.sem is op.sem:
            return
        if d.is_dma:
            if d not in op.deps:
                op.deps.append(d)
            return
        cur = op.cdeps.get(d.eng)
        if cur is None or d.idx > cur.idx:
            op.cdeps[d.eng] = d

    def add(self, eng, fn, r=(), w=(), dsem=None):
        op = Op(eng, fn, dsem is not None)
        op.idx = self.nops
        self.nops += 1
        if dsem is not None:
            dsem.n += 16
            op.sem = dsem.h
            op.val = dsem.n
        for k in r:
            lw = self.last_w.get(k)
            if lw is not None:
                self._add_dep(op, lw, True)
            if isinstance(k, str) and k.startswith("ps"):
                for rd in self.readers.get(k, ()):
                    if rd.eng != op.eng:
                        self._add_dep(op, rd, False)
        for k in w:
            lw = self.last_w.get(k)
            if lw is not None:
                self._add_dep(op, lw, False)
            for rd in self.readers.get(k, ()):
                self._add_dep(op, rd, False)
        for k in r:
            self.readers.setdefault(k, []).append(op)
        for k in w:
            self.last_w[k] = op
            self.readers[k] = []
        self.ops[eng].append(op)
        return op

    def dma(self, q, out, in_, r=(), w=(), sem=None):
        return self.add(q, lambda e: e.dma_start(out=out, in_=in_), r, w, dsem=self.dsem(sem, q))

    def mm(self, out, lhsT, rhs, start=True, stop=True, r=(), w=()):
        return self.add("pe", lambda e: e.matmul(out, lhsT=lhsT, rhs=rhs, start=start, stop=stop), r, w)

    def tr(self, out, in_, ident, r=(), w=()):
        return self.add("pe", lambda e: e.transpose(out, in_, ident), r, w)

    def act(self, out, in_, func, bias=None, scale=None, accum_out=None, r=(), w=()):
        kw = {}
        if bias is not None:
            kw["bias"] = bias
        if scale is not None:
            kw["scale"] = scale
        if accum_out is not None:
            kw["accum_out"] = accum_out
        return self.add("act", lambda e: e.activation(out=out, in_=in_, func=func, **kw), r, w)

    def tt(self, eng, out, in0, in1, op, r=(), w=()):
        return self.add(eng, lambda e: e.tensor_tensor(out=out, in0=in0, in1=in1, op=op), r, w)

    def ts(self, eng, out, in0, s1, op0, s2=None, op1=None, r=(), w=()):
        if op1 is None:
            return self.add(eng, lambda e: e.tensor_scalar(out=out, in0=in0, scalar1=s1, scalar2=None, op0=op0), r, w)
        return self.add(eng, lambda e: e.tensor_scalar(out=out, in0=in0, scalar1=s1, scalar2=s2, op0=op0, op1=op1), r, w)

    def stt(self, eng, out, in0, scalar, in1, op0, op1, r=(), w=()):
        return self.add(eng, lambda e: e.scalar_tensor_tensor(out=out, in0=in0, scalar=scalar, in1=in1, op0=op0, op1=op1), r, w)

    def copy(self, eng, out, in_, r=(), w=()):
        if eng == "act":
            return self.add("act", lambda e: e.activation(out=out, in_=in_, func=AF.Copy), r, w)
        return self.add(eng, lambda e: e.tensor_copy(out=out, in_=in_), r, w)

    def memset(self, eng, ap, val, w=()):
        return self.add(eng, lambda e: e.memset(ap, val), (), w)

    def recip(self, out, in_, r=(), w=()):
        return self.add("dve", lambda e: e.reciprocal(out=out, in_=in_), r, w)

    def emit(self, name):
        nc = self.nc
        for e in ENGS:
            for op in self.ops[e]:
                for d in op.cdeps.values():
                    d.marked = True
                    op.deps.append(d)
        for e in ("pe", "act", "dve", "pool"):
            cnt = self.eng_cnt[e]
            for op in self.ops[e]:
                if not op.is_dma and op.marked:
                    cnt += 1
                    op.sem = self.eng_sem[e]
                    op.val = cnt
            self.eng_cnt[e] = cnt
        final_dma = {}
        for e in ENGS:
            for op in self.ops[e]:
                if op.is_dma:
                    final_dma[op.sem] = max(final_dma.get(op.sem, 0), op.val)
        ops = self.ops

        def body(eobj, lst, is_sp):
            waited = {}
            for op in lst:
                for d in op.deps:
                    if waited.get(d.sem, 0) >= d.val:
                        continue
                    eobj.wait_ge(d.sem, d.val)
                    waited[d.sem] = d.val
                ins = op.fn(eobj)
                if op.is_dma:
                    ins.then_inc(op.sem, 16)
                elif op.marked:
                    ins.then_inc(op.sem, 1)
            if is_sp:
                for sem, val in final_dma.items():
                    eobj.wait_ge(sem, val)

        with nc.Block(name) as block:
            block.tensor(lambda e: body(e, ops["pe"], False))
            block.scalar(lambda e: body(e, ops["act"], False))
            block.vector(lambda e: body(e, ops["dve"], False))
            block.gpsimd(lambda e: body(e, ops["pool"], False))
            block.sync(lambda e: body(e, ops["sp"], True))
        for ds in self.dma_sems.values():
            self.free_dsems[ds.kind].append(ds)
        self.dma_sems = {}
        self.reset()


class Ctx:
    def __init__(self, nc):
        self.nc = nc
        self.P = Planner(nc)
        self.dram = {}
        self.bank = 0

    def dt(self, name, shape, dtype, kind=None):
        if kind is None:
            t = self.nc.dram_tensor(name, list(shape), dtype)
        else:
            t = self.nc.dram_tensor(name, list(shape), dtype, kind=kind)
        self.dram[name] = t.ap()
        return self.dram[name]

    def next_bank(self):
        b = self.bank
        self.bank = (b + 1) % 8
        return b


def rope_tables():
    t = np.arange(S)
    row = (t // 64).astype(np.float32)
    col = (t % 64).astype(np.float32)
    half = HD // 2
    inv = (np.float32(10000.0) ** (-np.arange(0, half, 2, dtype=np.float32) / np.float32(half))).astype(np.float32)

    def tab(pos):
        ang = pos[:, None] * inv[None, :]
        ang = np.concatenate([ang, ang], axis=-1).astype(np.float32)
        return np.cos(ang).astype(np.float32), np.sin(ang).astype(np.float32)

    cr, sr = tab(row)
    cc, sc = tab(col)
    cos = np.concatenate([cr, cc], axis=-1)
    sin = np.concatenate([sr, sc], axis=-1)
    sgn = np.tile(np.concatenate([-np.ones(16, np.float32), np.ones(16, np.float32)]), 2)
    sinS = sin * sgn[None, :]
    swap = np.arange(64).reshape(2, 2, 16)[:, ::-1, :].reshape(64)
    sinSw = sinS[:, swap]
    return np.ascontiguousarray(cos), np.ascontiguousarray(sinSw)


def const_layout():
    off = {}
    o = 0

    def put(name, n):
        nonlocal o
        off[name] = (o, n)
        o += n

    for L in range(DEPTH):
        put(f"ln1_{L}", 8)
        put(f"ln2_{L}", 8)
        put(f"qw_{L}", 64)
        put(f"kw_{L}", 64)
        put(f"dtb_{L}", 64)
        put(f"alog_{L}", 64)
        put(f"convw_{L}", 24 * 5)
        put(f"convb_{L}", 24)
        put(f"dskip_{L}", 32)
        put(f"ssmw_{L}", 16)
    put("fnw", 1024)
    return off, o


def pack_consts(inp):
    off, n = const_layout()
    c = np.zeros((128, n), np.float32)

    def setc(name, arr):
        o, m = off[name]
        c[:, o:o + m] = arr.reshape(128, m)

    rep = lambda v: np.broadcast_to(np.asarray(v, np.float32).reshape(1, -1), (128, np.asarray(v).size))
    for L in range(DEPTH):
        setc(f"ln1_{L}", np.asarray(inp["ln1_w"][L]).reshape(8, 128).T)
        setc(f"ln2_{L}", np.asarray(inp["ln2_w"][L]).reshape(8, 128).T)
        setc(f"qw_{L}", rep(inp["q_norm_w"][L]))
        setc(f"kw_{L}", rep(inp["k_norm_w"][L]))
        setc(f"dtb_{L}", rep(np.concatenate([inp["dt_bias_fwd"][L], inp["dt_bias_bwd"][L]])))
        setc(f"alog_{L}", rep(np.concatenate([inp["a_log_fwd"][L], inp["a_log_bwd"][L]])))
        cw = np.asarray(inp["conv_w"][L]).reshape(5, 24, 128)
        setc(f"convw_{L}", np.ascontiguousarray(cw.transpose(2, 1, 0)))
        setc(f"convb_{L}", np.asarray(inp["conv_b"][L]).reshape(24, 128).T)
        setc(f"dskip_{L}", rep(inp["d_skip"][L]))
        setc(f"ssmw_{L}", np.asarray(inp["ssm_norm_w"][L]).reshape(16, 128).T)
    setc("fnw", rep(inp["final_norm_w"]))
    return c


def fixed_consts():
    ident = np.eye(128, dtype=np.float32)
    j = np.arange(128)[:, None]
    l = np.arange(128)[None, :]
    U = (j <= l).astype(np.float32)
    Lm = (j >= l).astype(np.float32)
    ones = np.ones((128, 128), np.float32)
    tri = np.concatenate([ident, U, Lm, ones], axis=1)
    mf = np.where(l < j, -BIG, 0.0).astype(np.float32)
    mb = np.where(l > j, -BIG, 0.0).astype(np.float32)
    ind = np.zeros((64, 32, 128), np.float32)
    for hl in range(2):
        for h in range(32):
            ind[hl * 32 + h, h, :] = 1.0
    bfc = np.zeros((128, 128 + 128 + 128 + 4096), np.float32)
    bfc[:, 0:128] = ident
    bfc[:, 128:256] = mf
    bfc[:, 256:384] = mb
    bfc[0:64, 384:384 + 4096] = ind.reshape(64, 4096)
    bfc[64:128, 384:384 + 4096] = 1.0
    return tri, bfc.astype(ml_dtypes.bfloat16)


def phase_a1(C, L, x_src):
    nc, P, dr = C.nc, C.P, C.dram
    coff, _ = const_layout()
    with ExitStack() as es:
        def sb(name, shape, dt):
            return es.enter_context(nc.sbuf_tensor(f"a1{L}_{name}", shape, dt))

        ps = es.enter_context(nc.psum_tensor(f"a1{L}_ps", [128, 6, 512], F32))
        psb = es.enter_context(nc.psum_tensor(f"a1{L}_psb", [128, 2, 1024], BF16))
        cst = sb("cst", [128, const_layout()[1]], F32)
        tri = sb("tri", [128, 512], F32)
        identb = sb("identb", [128, 128], BF16)
        wbf = sb("wbf", [128, 8, 3648], BF16)
        xs = [sb(f"x{i}", [128, 1024], F32) for i in range(3)]
        junk = sb("junk", [128, 1024], BF16)
        ss = [sb(f"ss{i}", [128, 4], F32) for i in range(2)]
        hbf = [sb(f"h{i}", [128, 1024], BF16) for i in range(2)]
        hT = [sb(f"hT{i}", [128, 8, 512], BF16) for i in range(2)]
        cosb = [sb(f"cos{i}", [128, 64], F32) for i in range(3)]
        sinb = [sb(f"sin{i}", [128, 64], F32) for i in range(3)]
        cw = [sb(f"cw{i}", [128, 2, 64], F32) for i in range(2)]
        sw = [sb(f"sw{i}", [128, 2, 64], F32) for i in range(2)]
        sq = sb("sq", [128, 512], F32)
        ssq = sb("ssq", [128, 3, 8], F32)
        rstd = sb("rstd", [128, 3, 8], F32)
        xn = sb("xn", [128, 512], F32)
        t1 = sb("t1", [128, 512], F32)
        u = sb("u", [128, 512], F32)
        kbf = sb("kbf", [128, 256], BF16)
        qkbf = [sb(f"qkbf{i}", [128, 12, 128], BF16) for i in range(2)]
        qTs = [sb(f"qTs{i}", [128, 8, 512], BF16) for i in range(2)]
        kTs = [sb(f"kTs{i}", [128, 4, 512], BF16) for i in range(2)]
        vst = [sb(f"vst{i}", [128, 4, 256], BF16) for i in range(2)]
        zst = [sb(f"zst{i}", [128, 2048], BF16) for i in range(2)]
        abc = sb("abc", [128, 64], F32)
        dtb = sb("dtb", [128, 64], F32)
        e1 = sb("e1", [128, 64], F32)
        dtv = sb("dtv", [128, 64], F32)
        lndt = sb("lndt", [128, 64], F32)
        a32 = sb("a32", [128, 64], F32)
        csn = sb("csn", [128, 128], F32)
        dfx = sb("dfx", [128, 64], F32)
        exx = sb("exx", [128, 64], F32)
        scal = [sb(f"scal{i}", [128, 128], F32) for i in range(2)]
        cdt = [sb(f"cdt{i}", [128, 64], F32) for i in range(2)]
        trh = [sb(f"trh{i}", [128, 128], BF16) for i in range(2)]
        trl = [sb(f"trl{i}", [128, 128], BF16) for i in range(2)]
        epsb = sb("epsb", [128, 4], F32)
        P.memset("dve", epsb[:, 0:1], NORM_EPS, w=["epsb"])
        P.memset("dve", epsb[:, 1:2], QK_EPS, w=["epsb"])
        P.memset("dve", epsb[:, 2:3], 1.0, w=["epsb"])

        def cs_(name):
            o, n = coff[name]
            return cst[:, o:o + n]

        P.dma("sp", cst[:], dr["consts"], w=["cst"], sem="a1cst")
        P.dma("sp", tri[:], dr["tri"], w=["tri"], sem="a1tri")
        P.dma("sp", identb[:], dr["bfc"][:, 0:128], w=["identb"], sem="a1id")
        win = dr["w_in"]
        for cbk in range(8):
            c0 = cbk * 512
            n = 512 if cbk < 7 else 64
            src0 = c0 if cbk < 7 else 6656
            for k in range(8):
                P.dma("pool", wbf[:, k, c0:c0 + n], win[L, k * 128:(k + 1) * 128, src0:src0 + n], w=[f"wb{cbk}"], sem=f"a1wb{cbk}")
        P.act(abc[:], cs_(f"alog_{L}"), AF.Exp, r=["cst"], w=["abc"])
        P.ts("dve", abc[:], abc[:], -1.0, ALU.mult, r=["abc"], w=["abc"])

        ident_f = tri[:, 0:128]
        fb = 0
        bb = 0

        def nb():
            nonlocal fb
            b = fb
            fb = (fb + 1) % 6
            return b

        def nbb():
            nonlocal bb
            b = bb
            bb = (bb + 1) % 2
            return b

        NTT = DBG.get("a1_tiles", NT)

        def loads(t):
            x3 = t % 3
            P.dma("sp", xs[x3][:], x_src[t * 128:(t + 1) * 128, :], w=[f"x{x3}"], sem=f"a1x{x3}")
            P.dma("sp", cosb[x3][:], dr["rope_cos"][t * 128:(t + 1) * 128, :], w=[f"cos{x3}"], sem=f"a1cos{x3}")
            P.dma("sp", sinb[x3][:], dr["rope_sin"][t * 128:(t + 1) * 128, :], w=[f"sin{x3}"], sem=f"a1sin{x3}")

        def front_a(t):
            s2 = t % 2
            ch = t // 4
            cs2 = ch % 2
            j = t % 4
            x3 = t % 3
            if t == 0:
                loads(0)
                loads(1)
            if t + 2 < NTT:
                loads(t + 2)
            P.act(junk[:], xs[x3][:], AF.Square, accum_out=ss[s2][:, 0:1], r=[f"x{x3}"], w=["junk", f"ss{s2}"])
            P.act(ss[s2][:, 1:2], ss[s2][:, 0:1], AF.Ln, scale=1.0 / D, bias=epsb[:, 0:1], r=[f"ss{s2}", "epsb"], w=[f"ssb{s2}"])
            P.act(ss[s2][:, 2:3], ss[s2][:, 1:2], AF.Exp, scale=-0.5, r=[f"ssb{s2}"], w=[f"ssc{s2}"])
            P.act(hbf[s2][:], xs[x3][:], AF.Copy, scale=ss[s2][:, 2:3], r=[f"x{x3}", f"ssc{s2}"], w=[f"h{s2}"])
            P.tt("dve", cw[s2][:, 0, :], cosb[x3][:], cs_(f"qw_{L}"), ALU.mult, r=[f"cos{x3}", "cst"], w=[f"cw{s2}"])
            P.tt("dve", cw[s2][:, 1, :], cosb[x3][:], cs_(f"kw_{L}"), ALU.mult, r=[f"cos{x3}", "cst"], w=[f"cw{s2}"])
            P.tt("dve", sw[s2][:, 0, :], sinb[x3][:], cs_(f"qw_{L}"), ALU.mult, r=[f"sin{x3}", "cst"], w=[f"sw{s2}"])
            P.tt("dve", sw[s2][:, 1, :], sinb[x3][:], cs_(f"kw_{L}"), ALU.mult, r=[f"sin{x3}", "cst"], w=[f"sw{s2}"])

        def front_b(t):
            s2 = t % 2
            ch = t // 4
            cs2 = ch % 2
            j = t % 4
            b = nbb()
            for k in range(8):
                P.tr(psb[:, b, k * 128:(k + 1) * 128], hbf[s2][:, k * 128:(k + 1) * 128], identb[:],
                     r=[f"h{s2}", "identb"], w=[f"psb{b}"])
            lnw = cs_(f"ln1_{L}")
            P.tt("dve", hT[cs2][:, :, j * 128:(j + 1) * 128],
                 psb[:, b, :].rearrange("p (k t) -> p k t", k=8),
                 lnw.unsqueeze(2).to_broadcast([128, 8, 128]), ALU.mult,
                 r=[f"psb{b}", "cst"], w=[f"hT{cs2}_{j}"])

        def back(t):
            s2 = t % 2
            ch = t // 4
            cs2 = ch % 2
            j = t % 4
            hTk = lambda k: hT[cs2][:, k, j * 128:(j + 1) * 128]
            hkey = f"hT{cs2}_{j}"
            do_ssd = "ssd" not in DBG.get("skip", ())
            if t + 1 < NTT:
                front_a(t + 1)
            def proj(c0, n):
                bk = nb()
                for k in range(8):
                    P.mm(ps[:, bk, 0:n], hTk(k), wbf[:, k, c0:c0 + n], start=(k == 0), stop=(k == 7),
                         r=[hkey, f"wb{c0 // 512}"], w=[f"ps{bk}"])
                return bk

            for gi in range(0 if "qk" in DBG.get("skip", ()) else 3):
                nh = 8 if gi < 2 else 4
                wi = 0 if gi < 2 else 1
                bk = proj(gi * 512, 512)
                if gi == 0 and t >= 1 and do_ssd:
                    ssd2a(t - 1)
                n = nh * 64
                X = ps[:, bk, 0:n]
                QS = DBG.get("qk_stop", 99)
                P.act(sq[:, 0:n], X, AF.Square, r=[f"ps{bk}"], w=["sq"])
                if QS < 2:
                    return
                P.add("dve", lambda e, o=ssq[:, gi, 0:nh], i=sq[:, 0:n].rearrange("p (h d) -> p h d", d=64):
                      e.tensor_reduce(out=o, in_=i, axis=AX.X, op=ALU.add), r=["sq"], w=["ssq"])
                if QS < 3:
                    return
                P.act(rstd[:, gi, 0:nh], ssq[:, gi, 0:nh], AF.Ln, scale=1.0 / HD, bias=epsb[:, 1:2], r=["ssq", "epsb"], w=["rstd"])
                P.act(rstd[:, gi, 0:nh], rstd[:, gi, 0:nh], AF.Exp, scale=-0.5, r=["rstd"], w=["rstd"])
                if QS < 4:
                    return
                P.tt("dve", xn[:, 0:n].rearrange("p (h d) -> p h d", d=64), X.rearrange("p (h d) -> p h d", d=64),
                     rstd[:, gi, 0:nh].unsqueeze(2).to_broadcast([128, nh, 64]), ALU.mult,
                     r=[f"ps{bk}", "rstd"], w=["xn"])
                if QS < 5:
                    return
                P.tt("dve", t1[:, 0:n].rearrange("p (h d) -> p h d", d=64), xn[:, 0:n].rearrange("p (h d) -> p h d", d=64),
                     cw[s2][:, wi, :].unsqueeze(1).to_broadcast([128, nh, 64]), ALU.mult, r=["xn", f"cw{s2}"], w=["t1"])
                P.tt("dve", u[:, 0:n].rearrange("p (h d) -> p h d", d=64), xn[:, 0:n].rearrange("p (h d) -> p h d", d=64),
                     sw[s2][:, wi, :].unsqueeze(1).to_broadcast([128, nh, 64]), ALU.mult, r=["xn", f"sw{s2}"], w=["u"])
                if QS < 6:
                    return
                t1v = t1[:, 0:n].rearrange("p (a b d) -> p a b d", b=2, d=16)
                uv = u[:, 0:n].rearrange("p (a b d) -> p a b d", b=2, d=16)
                if gi < 2:
                    ov = qkbf[s2][:, gi * 4:(gi + 1) * 4, :].rearrange("p a (x b d) -> p (a x) b d", b=2, d=16)
                    okey = f"qkbf{s2}"
                else:
                    ov = kbf[:].rearrange("p (a b d) -> p a b d", b=2, d=16)
                    okey = "kbf"
                if "adds" not in DBG.get("skip", ()):
                    P.tt("dve", ov[:, :, 0, :], t1v[:, :, 0, :], uv[:, :, 1, :], ALU.add, r=["t1", "u"], w=[okey])
                    P.tt("dve", ov[:, :, 1, :], t1v[:, :, 1, :], uv[:, :, 0, :], ALU.add, r=["t1", "u"], w=[okey])
                if gi == 2 and "kdup" not in DBG.get("skip", ()):
                    kd = qkbf[s2][:, 8:12, :]
                    P.copy("pool", kd[:, :, 0:64], kbf[:].rearrange("p (g d) -> p g d", d=64), r=["kbf"], w=[f"qkbf{s2}"])
                    P.copy("pool", kd[:, :, 64:128], kbf[:].rearrange("p (g d) -> p g d", d=64), r=["kbf"], w=[f"qkbf{s2}"])
                if gi == 2 and "vcopy" not in DBG.get("skip", ()):
                    P.copy("act", vst[cs2][:, j, :], ps[:, bk, 256:512], r=[f"ps{bk}"], w=[f"vst{cs2}"])
            if t + 1 < NTT:
                front_b(t + 1)
            if t >= 1 and do_ssd:
                ssd2b(t - 1)
            for zi in range(0 if "z" in DBG.get("skip", ()) else 4):
                bk = proj(1536 + zi * 512, 512)
                P.act(zst[s2][:, zi * 512:(zi + 1) * 512], ps[:, bk, :], AF.Silu, r=[f"ps{bk}"], w=[f"zst{s2}"])
            P.dma("sp", dr["sz"][t * 128:(t + 1) * 128, :], zst[s2][:], r=[f"zst{s2}"], sem=f"a1zst{s2}")
            if "qk" in DBG.get("skip", ()) or DBG.get("qk_stop", 99) < 7:
                return
            b0 = nbb()
            for i in range(8):
                P.tr(psb[:, b0, i * 128:(i + 1) * 128], qkbf[s2][:, i, :], identb[:], r=[f"qkbf{s2}", "identb"], w=[f"psb{b0}"])
            P.copy("act", qTs[cs2][:, :, j * 128:(j + 1) * 128], psb[:, b0, :].rearrange("p (k t) -> p k t", k=8),
                   r=[f"psb{b0}"], w=[f"qTs{cs2}"])
            b1 = nbb()
            for i in range(4):
                P.tr(psb[:, b1, i * 128:(i + 1) * 128], qkbf[s2][:, 8 + i, :], identb[:], r=[f"qkbf{s2}", "identb"], w=[f"psb{b1}"])
            P.copy("act", kTs[cs2][:, :, j * 128:(j + 1) * 128], psb[:, b1, 0:512].rearrange("p (k t) -> p k t", k=4),
                   r=[f"psb{b1}"], w=[f"kTs{cs2}"])
            if j == 3:
                c0 = ch * 512
                P.dma("sp", dr["hT"].rearrange("(k p) t -> p k t", p=128)[:, :, c0:c0 + 512], hT[cs2][:],
                      r=[f"hT{cs2}_{jj}" for jj in range(4)], sem=f"a1hTst{cs2}")
                P.dma("sp", dr["qT"].rearrange("k p t -> p k t")[:, :, c0:c0 + 512], qTs[cs2][:], r=[f"qTs{cs2}"], sem=f"a1qTs{cs2}")
                P.dma("sp", dr["kT"].rearrange("k p t -> p k t")[:, :, c0:c0 + 512], kTs[cs2][:], r=[f"kTs{cs2}"], sem=f"a1kTs{cs2}")
                P.dma("sp", dr["v"][c0:c0 + 512, :].rearrange("(j p) c -> p j c", p=128), vst[cs2][:], r=[f"vst{cs2}"], sem=f"a1vst{cs2}")
            if "ssd" in DBG.get("skip", ()):
                return
            bk = proj(3584, 64)
            P.tt("dve", dtb[:], ps[:, bk, 0:64], cs_(f"dtb_{L}"), ALU.add, r=[f"ps{bk}", "cst"], w=["dtb"])
            P.act(e1[:], dtb[:], AF.Exp, r=["dtb"], w=["e1"])
            P.act(dtv[:], e1[:], AF.Ln, bias=epsb[:, 2:3], r=["e1", "epsb"], w=["dtv"])
            P.act(lndt[:], dtv[:], AF.Ln, r=["dtv"], w=["lndt"])
            P.tt("dve", a32[:], dtv[:], abc[:], ALU.mult, r=["dtv", "abc"], w=["a32"])

        def ssd2a(t):
            s2 = t % 2
            bk = nb()
            for i in range(3):
                P.mm(ps[:, bk, i * 64:(i + 1) * 64], tri[:, (i + 1) * 128:(i + 2) * 128], a32[:], r=["tri", "a32"], w=[f"ps{bk}"])
            P.copy("dve", csn[:, 0:32], ps[:, bk, 0:32], r=[f"ps{bk}"], w=["csn"])
            P.copy("dve", csn[:, 32:64], ps[:, bk, 64 + 32:64 + 64], r=[f"ps{bk}"], w=["csn"])
            P.tt("dve", dfx[:], ps[:, bk, 128:192], csn[:, 0:64], ALU.subtract, r=[f"ps{bk}", "csn"], w=["dfx"])
            P.act(cdt[s2][:], ps[:, bk, 128:192], AF.Exp, r=[f"ps{bk}"], w=[f"cdt{s2}"])
            P.dma("sp", dr["cd"][t], cdt[s2][:], r=[f"cdt{s2}"], sem=f"a1cdt{s2}")
            P.act(scal[s2][:, 0:64], csn[:, 0:64], AF.Exp, r=["csn"], w=[f"scal{s2}"])
            P.act(exx[:], dfx[:], AF.Exp, r=["dfx"], w=["exx"])
            P.tt("dve", scal[s2][:, 64:128], dtv[:], exx[:], ALU.mult, r=["dtv", "exx"], w=[f"scal{s2}"])
            P.dma("sp", dr["scal"][t], scal[s2][:], r=[f"scal{s2}"], sem=f"a1scal{s2}")
            P.tt("dve", csn[:, 64:128], lndt[:], csn[:, 0:64], ALU.subtract, r=["lndt", "csn"], w=["csn"])

        def ssd2b(t):
            s2 = t % 2
            bk = nb()
            P.tr(ps[:, bk, 0:128], csn[:], ident_f, r=["csn", "tri"], w=[f"ps{bk}"])
            P.copy("act", trh[s2][:], ps[:, bk, 0:128], r=[f"ps{bk}"], w=[f"trh{s2}"])
            P.tt("dve", trl[s2][:], ps[:, bk, 0:128], trh[s2][:], ALU.subtract, r=[f"ps{bk}", f"trh{s2}"], w=[f"trl{s2}"])
            P.dma("sp", dr["trd"][t, 0], trh[s2][:], r=[f"trh{s2}"], sem=f"a1trh{s2}")
            P.dma("sp", dr["trd"][t, 1], trl[s2][:], r=[f"trl{s2}"], sem=f"a1trl{s2}")

        front_a(0)
        front_b(0)
        for t in range(NTT):
            back(t)
        if "ssd" not in DBG.get("skip", ()):
            ssd2a(NTT - 1)
            ssd2b(NTT - 1)
        P.emit(f"a1_{L}")


SCRATCH = {
    "hT": ([D, S], BF16),
    "qT": ([8, 128, S], BF16),
    "kT": ([4, 128, S], BF16),
    "v": ([S, 256], BF16),
    "sz": ([S, DI], BF16),
    "scal": ([NT, 128, 128], F32),
    "trd": ([NT, 2, 128, 128], BF16),
    "cd": ([NT, 128, 64], F32),
    "xbcT": ([CONV, S], BF16),
    "mixT": ([3 * D, S], BF16),
    "xtm": ([S, DI], BF16),
    "btm": ([S, 512], BF16),
    "y1": ([S, DI], F32),
    "x1": ([S, D], F32),
    "x2": ([S, D], F32),
}
NPDT = {F32: np.float32, BF16: ml_dtypes.bfloat16}


def build(phases, dbg_out=(), dbg_in=()):
    nc = bass.Bass("TRN2", target_bir_lowering=False)
    C = Ctx(nc)
    C.dt("x", [S, D], F32, "ExternalInput")
    C.dt("w_in", [DEPTH, D, WIN], F32, "ExternalInput")
    C.dt("w_out", [DEPTH, 3 * D, D], F32, "ExternalInput")
    C.dt("w_up", [DEPTH, D, DFF], F32, "ExternalInput")
    C.dt("w_down", [DEPTH, DFF, D], F32, "ExternalInput")
    C.dt("consts", [128, const_layout()[1]], F32, "ExternalInput")
    C.dt("tri", [128, 512], F32, "ExternalInput")
    C.dt("bfc", [128, 384 + 4096], BF16, "ExternalInput")
    C.dt("rope_cos", [S, 64], F32, "ExternalInput")
    C.dt("rope_sin", [S, 64], F32, "ExternalInput")
    C.dt("out", [S, D], F32, "ExternalOutput")
    for name, (shape, dt) in SCRATCH.items():
        kind = "ExternalOutput" if name in dbg_out else ("ExternalInput" if name in dbg_in else None)
        C.dt(name, shape, dt, kind)
    for ph in phases:
        ph(C)
    return nc


def phase_a2(C, L):
    nc, P, dr = C.nc, C.P, C.dram
    coff, ncst = const_layout()
    with ExitStack() as es:
        def sb(name, shape, dt):
            return es.enter_context(nc.sbuf_tensor(f"a2{L}_{name}", shape, dt))

        ps = es.enter_context(nc.psum_tensor(f"a2{L}_ps", [128, 8, 512], F32))
        cst = sb("cst", [128, ncst], F32)
        hT = sb("hT", [128, 8, S], BF16)
        wx = sb("wx", [128, 8, CONV], BF16)
        raw = [sb(f"raw{i}", [128, S + 4], F32) for i in range(2)]
        segs = [(0, 2048, "dve"), (2048, 4096, "dve")]
        acc = [sb(f"acc{i}", [128, 2048], F32) for i in range(2)]
        outb = [sb(f"outb{i}", [128, S], BF16) for i in range(2)]

        P.dma("sp", cst[:], dr["consts"], w=["cst"], sem="a2cst")
        hsrc = dr["hT"].rearrange("(k p) t -> p k t", p=128)
        for i in range(8):
            P.dma("sp", hT[:, :, i * 512:(i + 1) * 512], hsrc[:, :, i * 512:(i + 1) * 512], w=[f"hT{i}"], sem=f"a2hT{i}")
        for cg in range(6):
            for k in range(8):
                P.dma("pool", wx[:, k, cg * 512:(cg + 1) * 512], dr["w_in"][L, k * 128:(k + 1) * 128, 3584 + cg * 512:3584 + (cg + 1) * 512],
                      w=[f"wx{cg}"], sem=f"a2wx{cg}")
        for i in range(2):
            P.memset("dve", raw[i][:, 0:2], 0.0, w=[f"rawpad{i}"])
            P.memset("dve", raw[i][:, S + 2:S + 4], 0.0, w=[f"rawpad{i}"])
        o_w, _ = coff[f"convw_{L}"]
        o_b, _ = coff[f"convb_{L}"]
        bkc = [0]
        NCC = DBG.get("a2_chunks", 24)

        def mmcopy(c):
            s2 = c % 2
            rs = raw[s2]
            for i in range(8):
                bk = bkc[0]
                for k in range(8):
                    P.mm(ps[:, bk, :], wx[:, k, c * 128:(c + 1) * 128], hT[:, k, i * 512:(i + 1) * 512],
                         start=(k == 0), stop=(k == 7), r=[f"wx{c // 4}", f"hT{i}"], w=[f"ps{bk}"])
                P.copy("act", rs[:, 2 + i * 512:2 + (i + 1) * 512], ps[:, bk, :], r=[f"ps{bk}"], w=[f"raw{s2}_{i}"])
                bkc[0] = (bk + 1) % 8

        def conv(c):
            s2 = c % 2
            rs = raw[s2]
            for si, (a, b, eng) in enumerate(segs):
                n = b - a
                rk = [f"raw{s2}_{i}" for i in range(max(0, (a - 2) // 512), min(7, (b + 1) // 512) + 1)] + [f"rawpad{s2}", "cst"]
                A = acc[si][:, 0:n]
                P.ts(eng, A, rs[:, a:a + n], cst[:, o_w + c * 5:o_w + c * 5 + 1], ALU.mult, r=rk, w=[f"acc{si}"])
                for j in range(1, 5):
                    wj = cst[:, o_w + c * 5 + j:o_w + c * 5 + j + 1]
                    P.stt(eng, A, rs[:, a + j:a + j + n], wj, A, ALU.mult, ALU.add, r=rk + [f"acc{si}"], w=[f"acc{si}"])
                P.act(outb[s2][:, a:b], A, AF.Silu, bias=cst[:, o_b + c:o_b + c + 1], r=[f"acc{si}", "cst"], w=[f"outb{s2}"])
            P.dma("sp", dr["xbcT"][c * 128:(c + 1) * 128, :], outb[s2][:], r=[f"outb{s2}"], sem=f"a2out{s2}")

        mmcopy(0)
        for c in range(NCC):
            if c + 1 < NCC:
                mmcopy(c + 1)
            conv(c)
        P.emit(f"a2_{L}")


def phase_attn(C, L):
    nc, P, dr = C.nc, C.P, C.dram
    with ExitStack() as es:
        def sb(name, shape, dt):
            return es.enter_context(nc.sbuf_tensor(f"at{L}_{name}", shape, dt))

        ps = es.enter_context(nc.psum_tensor(f"at{L}_ps", [128, 8, 512], F32))
        kT = sb("kT", [128, 4, S], BF16)
        vall = sb("vall", [128, NT, 4, 192], BF16)
        qT = [sb(f"qT{i}", [128, 8, 512], BF16) for i in range(2)]
        pT = [sb(f"pT{i}", [128, 2, 512], BF16) for i in range(3)]
        rcp = sb("rcp", [128, 512], F32)
        osb = [sb(f"osb{i}", [128, 512], F32) for i in range(4)]
        aT = [sb(f"aT{i}", [128, 8, 512], BF16) for i in range(2)]

        P.memset("pool", vall[:, :, :, 0:64], 1.0, w=["vall"])
        P.memset("pool", vall[:, :, :, 128:192], 1.0, w=["vall"])
        for g in range(4):
            P.dma("sp", kT[:, g, :], dr["kT"][g], w=[f"kT{g}"], sem=f"atk{g}")
        vsrc = dr["v"].rearrange("(t p) (g d) -> p t g d", p=128, d=64)
        for g in range(4):
            P.dma("sp", vall[:, :, g, 64:128], vsrc[:, :, g, :], r=[], w=["vall"], sem="atv")
        groups = [3] * 10 + [2]
        NQC = DBG.get("at_qc", 8)
        def ld_q(qc):
            q2 = qc % 2
            P.dma("sp", qT[q2][:], dr["qT"].rearrange("k p t -> p k t")[:, :, qc * 512:(qc + 1) * 512], w=[f"qT{q2}"], sem=f"atq{q2}")

        ld_q(0)
        for qc in range(NQC):
            q2 = qc % 2
            if qc + 1 < NQC:
                ld_q(qc + 1)
            for j in range(DBG.get("at_pairs", 8)):
                g = j // 2
                NU = NT // 2

                def s_mm(u):
                    for i in range(2):
                        kt = 2 * u + i
                        for hh in range(2):
                            st = (2 * u + hh) % 3
                            lo = 64 * hh
                            P.mm(ps[:, st * 2 + i, :], kT[lo:lo + 64, g, kt * 128:(kt + 1) * 128], qT[q2][lo:lo + 64, j, :],
                                 r=[f"kT{g}", f"qT{q2}"], w=[f"ps{st * 2 + i}"])

                def s_exp(u, hh):
                    st = (2 * u + hh) % 3
                    P.act(pT[st][:], ps[:, st * 2:st * 2 + 2, :], AF.Exp, scale=float(HD) ** -0.5,
                          r=[f"ps{st * 2}", f"ps{st * 2 + 1}"], w=[f"pT{st}"])

                def do_pv(u, hh):
                    st = (2 * u + hh) % 3
                    ob = 6 + hh
                    vlo = 64 if hh == 0 else 0
                    for i in range(2):
                        kt = 2 * u + i
                        P.mm(ps[:, ob, :], vall[:, kt, g, vlo:vlo + 128], pT[st][:, i, :],
                             start=(kt == 0), stop=(kt == NT - 1), r=["vall", f"pT{st}"], w=[f"ps{ob}"])

                for u in range(NU):
                    s_mm(u)
                    s_exp(u, 0)
                    if u >= 1:
                        do_pv(u - 1, 0)
                    s_exp(u, 1)
                    if u >= 1:
                        do_pv(u - 1, 1)
                do_pv(NU - 1, 0)
                do_pv(NU - 1, 1)
                for hh in range(2):
                    ob = 6 + hh
                    oi = (j % 2) * 2 + hh
                    P.copy("dve", osb[oi][:], ps[:, ob, :], r=[f"ps{ob}"], w=[f"osb{oi}"])
                for hh in range(2):
                    oi = (j % 2) * 2 + hh
                    olo, slo = (0, 64) if hh == 0 else (64, 0)
                    P.copy("dve", rcp[olo:olo + 64, :], osb[oi][slo:slo + 64, :], r=[f"osb{oi}"], w=[f"rcp{hh}"])
                    P.recip(rcp[olo:olo + 64, :], rcp[olo:olo + 64, :], r=[f"rcp{hh}"], w=[f"rcp{hh}"])
                    P.tt("dve", aT[q2][olo:olo + 64, j, :], osb[oi][olo:olo + 64, :], rcp[olo:olo + 64, :], ALU.mult,
                         r=[f"osb{oi}", f"rcp{hh}"], w=[f"aT{q2}"])
            P.dma("sp", dr["mixT"][0:D, :].rearrange("(j p) t -> p j t", p=128)[:, :, qc * 512:(qc + 1) * 512], aT[q2][:],
                  r=[f"aT{q2}"], sem=f"ataT{q2}")
        P.emit(f"attn_{L}")


def phase_sf(C, L):
    nc, P, dr = C.nc, C.P, C.dram
    coff, ncst = const_layout()
    with ExitStack() as es:
        def sb(name, shape, dt):
            return es.enter_context(nc.sbuf_tensor(f"sf{L}_{name}", shape, dt))

        ps = es.enter_context(nc.psum_tensor(f"sf{L}_ps", [128, 8, 512], F32))
        psT = ps[:, 4, :].bitcast(BF16)
        cst = sb("cst", [128, ncst], F32)
        bfc = sb("bfc", [128, 384], BF16)
        identb = bfc[:, 0:128]
        maskx = [sb(f"mask{d}", [128, 8, 128], BF16) for d in range(2)]
        diall = sb("diall", [128, 32, 128], BF16)
        xbc = [sb(f"xbc{i}", [128, 24, 256], BF16) for i in range(3)]
        l1 = [[sb(f"l1_{d}{i}", [128, 128], BF16) for i in range(3)] for d in range(2)]
        r1 = [[sb(f"r1_{d}{i}", [128, 4096], BF16) for i in range(3)] for d in range(2)]
        scal = [sb(f"scal{i}", [128, 128], F32) for i in range(3)]
        cdall = sb("cdall", [128, NT, 64], F32)
        E = [[sb(f"E{d}{i}", [128, 1024], BF16) for i in range(2)] for d in range(2)]
        T = sb("T", [128, 1024], BF16)
        M = [sb(f"M{i}", [128, 1024], BF16) for i in range(2)]
        cb = [sb(f"cb{i}", [128, 128], BF16) for i in range(2)]
        xtm = [sb(f"xtm{i}", [128, 2048], BF16) for i in range(2)]
        btm = [sb(f"btm{i}", [128, 512], BF16) for i in range(2)]
        xw = sb("xw", [128, 512], BF16)
        prev = sb("prev", [128, 2048], F32)
        prevb = sb("prevb", [128, 2048], BF16)
        ytmp = sb("ytmp", [128, 512], F32)
        y1 = [sb(f"y1{i}", [128, 2048], F32) for i in range(2)]

        P.dma("sp", cst[:], dr["consts"], w=["cst"], sem="sfcst")
        P.dma("sp", bfc[:], dr["bfc"][:, 0:384], w=["bfc"], sem="sfbfc")
        P.dma("sp", cdall[:], dr["cd"].rearrange("c p h -> p c h"), w=["cdall"], sem="sfcd")
        for d in range(2):
            P.copy("dve", maskx[d][:], bfc[:, 128 + d * 128:256 + d * 128].unsqueeze(1).to_broadcast([128, 8, 128]),
                   r=["bfc"], w=[f"mask{d}"])
            for i in range(3):
                P.memset("dve", l1[d][i][:], 0.0, w=[f"l1_{d}{i}"])
                P.memset("dve", r1[d][i][:], 0.0, w=[f"r1_{d}{i}"])
                P.dma("sp", l1[d][i][0:2, :], dr["bfc"][64:66, 384:512], w=[f"l1_{d}{i}"], sem=f"sfl1{d}{i}")
                P.dma("sp", r1[d][i][2:66, :], dr["bfc"][0:64, 384:4480], w=[f"r1_{d}{i}"], sem=f"sfr1{d}{i}")
        o_d, _ = coff[f"dskip_{L}"]
        P.tt("dve", diall[:], identb.unsqueeze(1).to_broadcast([128, 32, 128]),
             cst[:, o_d:o_d + 32].unsqueeze(2).to_broadcast([128, 32, 128]), ALU.mult, r=["bfc", "cst"], w=["diall"])
        P.memset("dve", prev[:], 0.0, w=[f"prev{g}" for g in range(NG)])
        P.memset("dve", prevb[:], 0.0, w=[f"prevb{g}" for g in range(NG)])

        xsrc = dr["xbcT"].rearrange("(k p) t -> p k t", p=128)
        NCH = DBG.get("ssd_chunks", NT)

        def loads(c):
            s3 = c % 3
            sc = (c // 2) % 3
            if c % 2 == 0:
                P.dma("sp", xbc[sc][:], xsrc[:, :, c * 128:c * 128 + 256], w=[f"xbc{sc}"], sem=f"sfxbc{sc}")
            P.dma("sp", scal[s3][:], dr["scal"][c], w=[f"scal{s3}"], sem=f"sfscal{s3}")
            for d in range(2):
                P.dma("sp", l1[d][s3][2:34, :], dr["trd"][c, 0, 64 + d * 32:96 + d * 32, :], w=[f"l1_{d}{s3}"], sem=f"sfl1{d}{s3}")
                P.dma("sp", l1[d][s3][34:66, :], dr["trd"][c, 1, 64 + d * 32:96 + d * 32, :], w=[f"l1_{d}{s3}"], sem=f"sfl1{d}{s3}")
                for hl in range(2):
                    P.dma("sp", r1[d][s3][hl:hl + 1, :],
                          dr["trd"][c, hl:hl + 1, d * 32:d * 32 + 32, :].rearrange("a h l -> a (h l)"),
                          w=[f"r1_{d}{s3}"], sem=f"sfr1{d}{s3}")

        def stage_a(c, g, sl):
            s2 = c % 2
            s3 = c % 3
            sc = (c // 2) % 3
            cc = c % 2
            if g == 0:
                if c == 0:
                    loads(0)
                if c + 1 < NCH:
                    loads(c + 1)
            tk = slice(cc * 128, (cc + 1) * 128)
            for i in range(4):
                P.tr(psT[:, i * 128:(i + 1) * 128], xbc[sc][:, 4 * g + i, tk], identb, r=[f"xbc{sc}", "bfc"], w=["ps4"])
            P.tr(psT[:, 512:640], xbc[sc][:, 16 + g, tk], identb, r=[f"xbc{sc}", "bfc"], w=["ps4"])
            P.mm(ps[:, 4, 320:448], xbc[sc][:, 16 + g, tk], xbc[sc][:, 20 + g, tk], r=[f"xbc{sc}"], w=["ps4"])
            P.copy("act", xtm[s2][:, g * 512:(g + 1) * 512], psT[:, 0:512], r=["ps4"], w=[f"xtm{s2}_{g}"])
            P.copy("act", btm[s2][:, g * 128:(g + 1) * 128], psT[:, 512:640], r=["ps4"], w=[f"btm{s2}_{g}"])
            P.copy("act", cb[sl][:], ps[:, 4, 320:448], r=["ps4"], w=[f"cb{sl}"])
            for d in range(2):
                b0 = 2 * d
                for hb in range(2):
                    cols = slice((g * 8 + hb * 4) * 128, (g * 8 + hb * 4 + 4) * 128)
                    P.mm(ps[:, b0 + hb, :], l1[d][s3][:], r1[d][s3][:, cols], start=True, stop=False,
                         r=[f"l1_{d}{s3}", f"r1_{d}{s3}"], w=[f"ps{b0 + hb}"])
                    P.mm(ps[:, b0 + hb, :], identb, maskx[d][:, hb * 4:hb * 4 + 4, :], start=False, stop=True,
                         r=["bfc", f"mask{d}"], w=[f"ps{b0 + hb}"])
                P.act(E[d][sl][:].rearrange("p (a b) -> p a b", a=2), ps[:, b0:b0 + 2, :], AF.Exp,
                      r=[f"ps{b0}", f"ps{b0 + 1}"], w=[f"E{d}{sl}"])

        def hdr(c):
            return c % 2, (c // 2) % 3, c % 3, slice((c % 2) * 128, (c % 2 + 1) * 128)

        def s2d(c, g, sl):
            s2, sc, s3, tk = hdr(c)
            P.tt("dve", T[:], E[0][sl][:], E[1][sl][:], ALU.add, r=[f"E0{sl}", f"E1{sl}"], w=["T"])
            P.tt("dve", M[sl][:].rearrange("p (h l) -> p h l", l=128), T[:].rearrange("p (h l) -> p h l", l=128),
                 cb[sl][:].unsqueeze(1).to_broadcast([128, 8, 128]), ALU.mult, r=["T", f"cb{sl}"], w=[f"M{sl}"])
            wf = scal[s3][:, 64 + g * 8:64 + g * 8 + 8]
            P.tt("dve", xw[:].rearrange("p (e d) -> p e d", d=64), xtm[s2][:, g * 512:(g + 1) * 512].rearrange("p (e d) -> p e d", d=64),
                 wf.unsqueeze(2).to_broadcast([128, 8, 64]), ALU.mult, r=[f"xtm{s2}_{g}", f"scal{s3}"], w=["xw"])

        def s2p(c, g, sl):
            s2, sc, s3, tk = hdr(c)
            for e in range(8):
                h = g * 8 + e
                xh = xtm[s2][:, g * 512 + e * 64:g * 512 + (e + 1) * 64]
                P.mm(ps[:, 6, e * 64:(e + 1) * 64], M[sl][:, e * 128:(e + 1) * 128], xh, start=True, stop=False,
                     r=[f"M{sl}", f"xtm{s2}_{g}"], w=["ps6"])
                P.mm(ps[:, 6, e * 64:(e + 1) * 64], diall[:, h, :], xh, start=False, stop=True,
                     r=["diall", f"xtm{s2}_{g}"], w=["ps6"])
            P.mm(ps[:, 7, :], xbc[sc][:, 20 + g, tk], prevb[:, g * 512:(g + 1) * 512], r=[f"xbc{sc}", f"prevb{g}"], w=["ps7"])
            P.mm(ps[:, 5, :], btm[s2][:, g * 128:(g + 1) * 128], xw[:], r=[f"btm{s2}_{g}", "xw"], w=["ps5"])

        def s3_(c, g, sl):
            s2, sc, s3, tk = hdr(c)
            ef = scal[s3][:, g * 8:g * 8 + 8]
            P.tt("dve", ytmp[:].rearrange("p (e d) -> p e d", d=64), ps[:, 7, :].rearrange("p (e d) -> p e d", d=64),
                 ef.unsqueeze(2).to_broadcast([128, 8, 64]), ALU.mult, r=["ps7", f"scal{s3}"], w=["ytmp"])
            P.tt("dve", y1[s2][:, g * 512:(g + 1) * 512], ytmp[:], ps[:, 6, :], ALU.add, r=["ytmp", "ps6"], w=[f"y1{s2}"])
            pg = prev[:, g * 512:(g + 1) * 512]
            cdf = cdall[:, c, g * 8:g * 8 + 8]
            P.tt("dve", pg.rearrange("p (e d) -> p e d", d=64), pg.rearrange("p (e d) -> p e d", d=64),
                 cdf.unsqueeze(2).to_broadcast([128, 8, 64]), ALU.mult, r=[f"prev{g}", "cdall"], w=[f"prev{g}"])
            P.tt("dve", pg, pg, ps[:, 5, :], ALU.add, r=[f"prev{g}", "ps5"], w=[f"prev{g}"])
            P.copy("act", prevb[:, g * 512:(g + 1) * 512], pg, r=[f"prev{g}"], w=[f"prevb{g}"])
            if g == NG - 1:
                rows = slice(c * 128, (c + 1) * 128)
                P.dma("sp", dr["xtm"][rows, :], xtm[s2][:], r=[f"xtm{s2}_{gg}" for gg in range(4)], sem=f"sfxtm{s2}")
                P.dma("sp", dr["btm"][rows, :], btm[s2][:], r=[f"btm{s2}_{gg}" for gg in range(4)], sem=f"sfbtm{s2}")
                P.dma("sp", dr["y1"][rows, :], y1[s2][:], r=[f"y1{s2}"], sem=f"sfy1{s2}")

        items = [(c, g) for c in range(NCH) for g in range(NG)]
        n_it = len(items)
        stage_a(*items[0], 0)
        if n_it > 1:
            stage_a(*items[1], 1)
        s2d(*items[0], 0)
        s2p(*items[0], 0)
        for k in range(n_it):
            if k + 2 < n_it:
                stage_a(*items[k + 2], (k + 2) % 2)
            if k + 1 < n_it:
                s2d(*items[k + 1], (k + 1) % 2)
            s3_(*items[k], k % 2)
            if k + 1 < n_it:
                s2p(*items[k + 1], (k + 1) % 2)
        P.emit(f"sf_{L}")


def phase_sb(C, L):
    nc, P, dr = C.nc, C.P, C.dram
    coff, ncst = const_layout()
    with ExitStack() as es:
        def sb(name, shape, dt):
            return es.enter_context(nc.sbuf_tensor(f"sb{L}_{name}", shape, dt))

        ps = es.enter_context(nc.psum_tensor(f"sb{L}_ps", [128, 8, 512], F32))
        cst = sb("cst", [128, ncst], F32)
        identb = sb("identb", [128, 128], BF16)
        cdall = sb("cdall", [128, NT, 64], F32)
        epsb = sb("epsb", [128, 1], F32)
        ct = [sb(f"ct{i}", [128, 4, 128], BF16) for i in range(3)]
        xtm = [sb(f"xtm{i}", [128, 2048], BF16) for i in range(3)]
        btm = [sb(f"btm{i}", [128, 512], BF16) for i in range(3)]
        y1 = [sb(f"y1{i}", [128, 2048], F32) for i in range(3)]
        sz = [sb(f"sz{i}", [128, 2048], BF16) for i in range(3)]
        scal = [sb(f"scal{i}", [128, 128], F32) for i in range(3)]
        xw = sb("xw", [128, 512], BF16)
        prev = sb("prev", [128, 2048], F32)
        prevb = sb("prevb", [128, 2048], BF16)
        ytmp = sb("ytmp", [128, 512], F32)
        yy = [sb(f"yy{i}", [128, 512], F32) for i in range(2)]
        gg = [sb(f"gg{i}", [128, 512], F32) for i in range(2)]
        junk = sb("junk", [128, 512], BF16)
        ssq = sb("ssq", [128, 4], F32)
        gn = [sb(f"gn{i}", [128, 512], BF16) for i in range(2)]
        ynT = [sb(f"ynT{i}", [128, 16, 512], BF16) for i in range(2)]

        P.dma("sp", cst[:], dr["consts"], w=["cst"], sem="sbcst")
        P.dma("sp", identb[:], dr["bfc"][:, 0:128], w=["identb"], sem="sbid")
        P.dma("sp", cdall[:], dr["cd"].rearrange("c p h -> p c h"), w=["cdall"], sem="sbcd")
        P.memset("dve", epsb[:], NORM_EPS, w=["epsb"])
        P.memset("dve", prev[:], 0.0, w=[f"prev{g}" for g in range(NG)])
        P.memset("dve", prevb[:], 0.0, w=[f"prevb{g}" for g in range(NG)])
        o_w, _ = coff[f"ssmw_{L}"]
        csrc = dr["xbcT"][2560:3072, :].rearrange("(g p) t -> p g t", p=128)
        NCH = DBG.get("ssd_chunks", NT)
        bank = [0]
        trb = {}

        def loads(ci):
            c = NCH - 1 - ci
            s2 = ci % 3
            rows = slice(c * 128, (c + 1) * 128)
            P.dma("sp", ct[s2][:], csrc[:, :, rows], w=[f"ct{s2}"], sem=f"sbct{s2}")
            P.dma("sp", xtm[s2][:], dr["xtm"][rows, :], w=[f"xtm{s2}"], sem=f"sbxtm{s2}")
            P.dma("sp", btm[s2][:], dr["btm"][rows, :], w=[f"btm{s2}"], sem=f"sbbtm{s2}")
            P.dma("sp", y1[s2][:], dr["y1"][rows, :], w=[f"y1{s2}"], sem=f"sby1{s2}")
            P.dma("sp", sz[s2][:], dr["sz"][rows, :], w=[f"sz{s2}"], sem=f"sbsz{s2}")
            P.dma("sp", scal[s2][:], dr["scal"][c], w=[f"scal{s2}"], sem=f"sbscal{s2}")

        def stage_a(ci, g, sl):
            c = NCH - 1 - ci
            s2 = ci % 3
            if g == 0:
                if ci == 0:
                    loads(0)
                if ci + 1 < NCH:
                    loads(ci + 1)
            gs = slice(g * 512, (g + 1) * 512)
            b_off, b_st = bank[0] % 8, (bank[0] + 1) % 8
            bank[0] += 2
            P.mm(ps[:, b_off, :], ct[s2][:, g, :], prevb[:, gs], r=[f"ct{s2}", f"prevb{g}"], w=[f"ps{b_off}"])
            wb = scal[s2][:, 96 + g * 8:96 + g * 8 + 8]
            P.tt("dve", xw[:].rearrange("p (e d) -> p e d", d=64), xtm[s2][:, gs].rearrange("p (e d) -> p e d", d=64),
                 wb.unsqueeze(2).to_broadcast([128, 8, 64]), ALU.mult, r=[f"xtm{s2}", f"scal{s2}"], w=["xw"])
            P.mm(ps[:, b_st, :], btm[s2][:, g * 128:(g + 1) * 128], xw[:], r=[f"btm{s2}", "xw"], w=[f"ps{b_st}"])
            eb = scal[s2][:, 32 + g * 8:32 + g * 8 + 8]
            P.tt("dve", ytmp[:].rearrange("p (e d) -> p e d", d=64), ps[:, b_off, :].rearrange("p (e d) -> p e d", d=64),
                 eb.unsqueeze(2).to_broadcast([128, 8, 64]), ALU.mult, r=[f"ps{b_off}", f"scal{s2}"], w=["ytmp"])
            P.tt("dve", yy[sl][:], ytmp[:], y1[s2][:, gs], ALU.add, r=["ytmp", f"y1{s2}"], w=[f"yy{sl}"])
            pg = prev[:, gs]
            cdb = cdall[:, c, 32 + g * 8:32 + g * 8 + 8]
            P.tt("dve", pg.rearrange("p (e d) -> p e d", d=64), pg.rearrange("p (e d) -> p e d", d=64),
                 cdb.unsqueeze(2).to_broadcast([128, 8, 64]), ALU.mult, r=[f"prev{g}", "cdall"], w=[f"prev{g}"])
            P.tt("dve", pg, pg, ps[:, b_st, :], ALU.add, r=[f"prev{g}", f"ps{b_st}"], w=[f"prev{g}"])
            P.copy("act", prevb[:, gs], pg, r=[f"prev{g}"], w=[f"prevb{g}"])

        def stage_b(ci, g, sl):
            c = NCH - 1 - ci
            s2 = ci % 3
            gs = slice(g * 512, (g + 1) * 512)
            P.tt("dve", gg[sl][:], yy[sl][:], sz[s2][:, gs], ALU.mult, r=[f"yy{sl}", f"sz{s2}"], w=[f"gg{sl}"])
            P.act(junk[:], gg[sl][:], AF.Square, accum_out=ssq[:, 0:1], r=[f"gg{sl}"], w=["junk", "ssq0"])
            P.act(ssq[:, 1:2], ssq[:, 0:1], AF.Ln, scale=1.0 / 512, bias=epsb[:, 0:1], r=["ssq0", "epsb"], w=["ssq1"])
            P.act(ssq[:, 2:3], ssq[:, 1:2], AF.Exp, scale=-0.5, r=["ssq1"], w=["ssq2"])
            P.act(gn[sl][:], gg[sl][:], AF.Copy, scale=ssq[:, 2:3], r=[f"gg{sl}", "ssq2"], w=[f"gn{sl}"])

        def stage_b1b(ci, g, sl):
            b_tr = bank[0] % 8
            bank[0] += 1
            psT = ps[:, b_tr, :].bitcast(BF16)
            for i in range(4):
                P.tr(psT[:, i * 128:(i + 1) * 128], gn[sl][:, i * 128:(i + 1) * 128], identb[:], r=[f"gn{sl}", "identb"], w=[f"ps{b_tr}"])
            trb[(ci, g)] = b_tr

        def stage_b2(ci, g, sl):
            c = NCH - 1 - ci
            q4 = c % 4
            oc = c // 4
            o2 = oc % 2
            b_tr = trb.pop((ci, g))
            psT = ps[:, b_tr, :].bitcast(BF16)
            P.copy("act", ynT[o2][:, 4 * g:4 * g + 4, q4 * 128:(q4 + 1) * 128], psT[:, 0:512].rearrange("p (k t) -> p k t", k=4),
                   r=[f"ps{b_tr}"], w=[f"ynT{o2}"])
            if q4 == 0 and g == NG - 1:
                P.dma("sp", dr["mixT"][D:3 * D, :].rearrange("(k p) t -> p k t", p=128)[:, :, oc * 512:(oc + 1) * 512], ynT[o2][:],
                      r=[f"ynT{o2}"], sem=f"sbyn{o2}")

        items = [(ci, g) for ci in range(NCH) for g in range(NG)]
        stage_a(*items[0], 0)
        for i, it in enumerate(items):
            if i + 1 < len(items):
                stage_a(*items[i + 1], (i + 1) % 2)
            stage_b(*it, i % 2)
            if i >= 1:
                stage_b1b(*items[i - 1], (i - 1) % 2)
            if i >= 2:
                stage_b2(*items[i - 2], (i - 2) % 2)
        n_it = len(items)
        stage_b1b(*items[-1], (n_it - 1) % 2)
        if n_it >= 2:
            stage_b2(*items[-2], (n_it - 2) % 2)
        stage_b2(*items[-1], (n_it - 1) % 2)
        P.emit(f"sb_{L}")


def phase_d1(C, L, x_src):
    nc, P, dr = C.nc, C.P, C.dram
    coff, ncst = const_layout()
    with ExitStack() as es:
        def sb(name, shape, dt):
            return es.enter_context(nc.sbuf_tensor(f"d1{L}_{name}", shape, dt))

        ps = es.enter_context(nc.psum_tensor(f"d1{L}_ps", [128, 6, 512], F32))
        psb = es.enter_context(nc.psum_tensor(f"d1{L}_psb", [128, 2, 1024], BF16))
        cst = sb("cst", [128, ncst], F32)
        identb = sb("identb", [128, 128], BF16)
        epsb = sb("epsb", [128, 1], F32)
        wo = sb("wo", [128, 24, D], BF16)
        mix = [sb(f"mix{i}", [128, 24, 512], BF16) for i in range(2)]
        xs = [sb(f"x{i}", [128, D], F32) for i in range(3)]
        x1 = [sb(f"x1{i}", [128, D], F32) for i in range(2)]
        junk = sb("junk", [128, D], BF16)
        ss = [sb(f"ss{i}", [128, 4], F32) for i in range(2)]
        hbf = [sb(f"h{i}", [128, D], BF16) for i in range(2)]
        hT = [sb(f"hT{i}", [128, 8, 512], BF16) for i in range(2)]

        P.dma("sp", cst[:], dr["consts"], w=["cst"], sem="d1cst")
        P.dma("sp", identb[:], dr["bfc"][:, 0:128], w=["identb"], sem="d1id")
        P.memset("dve", epsb[:], NORM_EPS, w=["epsb"])
        for hf in range(2):
            for k in range(24):
                P.dma("pool", wo[:, k, hf * 512:(hf + 1) * 512], dr["w_out"][L, k * 128:(k + 1) * 128, hf * 512:(hf + 1) * 512],
                      w=[f"wo{hf}_{k // 8}"], sem=f"d1wo{hf}_{k // 8}")
        o_l, _ = coff[f"ln2_{L}"]
        o_sw, _ = coff[f"ssmw_{L}"]
        for hf in range(2):
            for k in range(8, 24):
                P.ts("dve", wo[:, k, hf * 512:(hf + 1) * 512], wo[:, k, hf * 512:(hf + 1) * 512], cst[:, o_sw + k - 8:o_sw + k - 7], ALU.mult,
                     r=[f"wo{hf}_{k // 8}", "cst"], w=[f"wo{hf}_{k // 8}"])
        msrc = dr["mixT"].rearrange("(k p) t -> p k t", p=128)
        fb = 0
        NTT = DBG.get("d_tiles", NT)

        def ld_mix(ch):
            c2 = ch % 2
            for k3 in range(3):
                P.dma("sp", mix[c2][:, k3 * 8:(k3 + 1) * 8, :], msrc[:, k3 * 8:(k3 + 1) * 8, ch * 512:(ch + 1) * 512],
                      w=[f"mix{c2}_{k3}"], sem=f"d1mix{c2}_{k3}")

        def ld_x(t):
            x3 = t % 3
            P.dma("sp", xs[x3][:], x_src[t * 128:(t + 1) * 128, :], w=[f"x{x3}"], sem=f"d1x{x3}")

        bbk = [0]

        def trans(t):
            s2 = t % 2
            ch = t // 4
            c2 = ch % 2
            j = t % 4
            b = bbk[0]
            bbk[0] = (b + 1) % 2
            for k in range(8):
                P.tr(psb[:, b, k * 128:(k + 1) * 128], hbf[s2][:, k * 128:(k + 1) * 128], identb[:], r=[f"h{s2}", "identb"], w=[f"psb{b}"])
            P.tt("dve", hT[c2][:, :, j * 128:(j + 1) * 128], psb[:, b, :].rearrange("p (k t) -> p k t", k=8),
                 cst[:, o_l:o_l + 8].unsqueeze(2).to_broadcast([128, 8, 128]), ALU.mult, r=[f"psb{b}", "cst"], w=[f"hT{c2}_{j}"])
            if j == 3:
                P.dma("sp", dr["hT"].rearrange("(k p) t -> p k t", p=128)[:, :, ch * 512:(ch + 1) * 512], hT[c2][:],
                      r=[f"hT{c2}_{jj}" for jj in range(4)], sem=f"d1hT{c2}")

        for t in range(NTT):
            s2 = t % 2
            ch = t // 4
            c2 = ch % 2
            j = t % 4
            if t == 0:
                ld_mix(0)
                ld_x(0)
                ld_x(1)
            if j == 0 and ch + 1 < NTT // 4:
                ld_mix(ch + 1)
            if t + 2 < NTT:
                ld_x(t + 2)
            x3 = t % 3
            for hf in range(2):
                bk = fb
                fb = (fb + 1) % 6
                for k in range(24):
                    P.mm(ps[:, bk, :], mix[c2][:, k, j * 128:(j + 1) * 128], wo[:, k, hf * 512:(hf + 1) * 512],
                         start=(k == 0), stop=(k == 23), r=[f"mix{c2}_{k // 8}", f"wo{hf}_{k // 8}"], w=[f"ps{bk}"])
                P.tt("dve", x1[s2][:, hf * 512:(hf + 1) * 512], ps[:, bk, :], xs[x3][:, hf * 512:(hf + 1) * 512], ALU.add,
                     r=[f"ps{bk}", f"x{x3}"], w=[f"x1{s2}_{hf}"])
            P.dma("sp", dr["x1"][t * 128:(t + 1) * 128, :], x1[s2][:], r=[f"x1{s2}_0", f"x1{s2}_1"], sem=f"d1x1{s2}")
            xk = [f"x1{s2}_0", f"x1{s2}_1"]
            P.act(junk[:], x1[s2][:], AF.Square, accum_out=ss[s2][:, 0:1], r=xk, w=["junk", f"ss{s2}"])
            P.act(ss[s2][:, 1:2], ss[s2][:, 0:1], AF.Ln, scale=1.0 / D, bias=epsb[:, 0:1], r=[f"ss{s2}", "epsb"], w=[f"ssb{s2}"])
            P.act(ss[s2][:, 2:3], ss[s2][:, 1:2], AF.Exp, scale=-0.5, r=[f"ssb{s2}"], w=[f"ssc{s2}"])
            P.act(hbf[s2][:], x1[s2][:], AF.Copy, scale=ss[s2][:, 2:3], r=xk + [f"ssc{s2}"], w=[f"h{s2}"])
            if t >= 1:
                trans(t - 1)
        trans(NTT - 1)
        P.emit(f"d1_{L}")


def phase_d2(C, L, last):
    nc, P, dr = C.nc, C.P, C.dram
    coff, ncst = const_layout()
    with ExitStack() as es:
        def sb(name, shape, dt):
            return es.enter_context(nc.sbuf_tensor(f"d2{L}_{name}", shape, dt))

        ps = es.enter_context(nc.psum_tensor(f"d2{L}_ps", [128, 8, 512], F32))
        cst = sb("cst", [128, ncst], F32)
        epsb = sb("epsb", [128, 1], F32)
        wu = sb("wu", [128, 8, DFF], BF16)
        wd = sb("wd", [128, 32, D], BF16)
        hT = [sb(f"hT{i}", [128, 8, 512], BF16) for i in range(2)]
        uT = sb("uT", [128, 32, 512], BF16)
        rl = [sb(f"rl{i}", [128, 512], F32) for i in range(2)]
        x1 = [sb(f"x1{i}", [128, D], F32) for i in range(3)]
        x2 = [sb(f"x2{i}", [128, D], F32) for i in range(2)]
        ss = [sb(f"ss{i}", [128, 4], F32) for i in range(2)]
        P.dma("sp", cst[:], dr["consts"], w=["cst"], sem="d2cst")
        P.memset("dve", epsb[:], NORM_EPS, w=["epsb"])
        for fg in range(8):
            for k in range(8):
                P.dma("pool", wu[:, k, fg * 512:(fg + 1) * 512], dr["w_up"][L, k * 128:(k + 1) * 128, fg * 512:(fg + 1) * 512],
                      w=[f"wu{fg}"], sem=f"d2wu{fg}")
        for f in range(32):
            P.dma("pool", wd[:, f, :], dr["w_down"][L, f * 128:(f + 1) * 128, :], w=[f"wd{f // 4}"], sem=f"d2wd{f // 4}")
        o_f, _ = coff["fnw"]
        hsrc = dr["hT"].rearrange("(k p) t -> p k t", p=128)
        bk = 0
        NCHK = DBG.get("d_tiles", NT) // 4

        def ld_h(ch):
            c2 = ch % 2
            P.dma("sp", hT[c2][:], hsrc[:, :, ch * 512:(ch + 1) * 512], w=[f"hT{c2}"], sem=f"d2hT{c2}")

        def ld_x1(t):
            x3 = t % 3
            P.dma("sp", x1[x3][:], dr["x1"][t * 128:(t + 1) * 128, :], w=[f"x1{x3}"], sem=f"d2x1{x3}")

        ld_h(0)
        ld_x1(0)
        ld_x1(1)
        for ch in range(NCHK):
            c2 = ch % 2
            if ch + 1 < NCHK:
                ld_h(ch + 1)
            for f in range(32):
                for k in range(8):
                    P.mm(ps[:, bk, :], wu[:, k, f * 128:(f + 1) * 128], hT[c2][:, k, :], start=(k == 0), stop=(k == 7),
                         r=[f"wu{f // 4}", f"hT{c2}"], w=[f"ps{bk}"])
                r2 = f % 2
                P.act(rl[r2][:], ps[:, bk, :], AF.Relu, r=[f"ps{bk}"], w=[f"rl{r2}"])
                P.tt("dve", uT[:, f, :], rl[r2][:], rl[r2][:], ALU.mult, r=[f"rl{r2}"], w=[f"uT{f}"])
                bk = (bk + 1) % 8
            for j in range(4):
                t = ch * 4 + j
                s2 = t % 2
                x3 = t % 3
                if t + 2 < NCHK * 4:
                    ld_x1(t + 2)
                for hf in range(2):
                    for f in range(32):
                        P.mm(ps[:, bk, :], uT[:, f, j * 128:(j + 1) * 128], wd[:, f, hf * 512:(hf + 1) * 512],
                             start=(f == 0), stop=(f == 31), r=[f"uT{f}", f"wd{f // 4}"], w=[f"ps{bk}"])
                    P.tt("dve", x2[s2][:, hf * 512:(hf + 1) * 512], ps[:, bk, :], x1[x3][:, hf * 512:(hf + 1) * 512], ALU.add,
                         r=[f"ps{bk}", f"x1{x3}"], w=[f"x2{s2}_{hf}"])
                    bk = (bk + 1) % 8
                xk = [f"x2{s2}_0", f"x2{s2}_1"]
                if not last:
                    P.dma("sp", dr["x2"][t * 128:(t + 1) * 128, :], x2[s2][:], r=xk, sem=f"d2x2{s2}")
                else:
                    P.act(rl[0][:].bitcast(BF16), x2[s2][:], AF.Square, accum_out=ss[s2][:, 0:1], r=xk, w=["rl0", f"ss{s2}"])
                    P.act(ss[s2][:, 1:2], ss[s2][:, 0:1], AF.Ln, scale=1.0 / D, bias=epsb[:, 0:1], r=[f"ss{s2}", "epsb"], w=[f"ssb{s2}"])
                    P.act(ss[s2][:, 2:3], ss[s2][:, 1:2], AF.Exp, scale=-0.5, r=[f"ssb{s2}"], w=[f"ssc{s2}"])
                    P.stt("dve", x2[s2][:], x2[s2][:], ss[s2][:, 2:3], cst[:, o_f:o_f + D], ALU.mult, ALU.mult,
                          r=xk + [f"ssc{s2}", "cst"], w=xk)
                    P.dma("sp", dr["out"][t * 128:(t + 1) * 128, :], x2[s2][:], r=xk, sem=f"d2ob{s2}")
        P.emit(f"d2_{L}")


def all_phases():
    phs = []
    for L in range(DEPTH):
        xin = "x" if L == 0 else "x2"
        phs.append(lambda C, L=L, xin=xin: phase_a1(C, L, C.dram[xin]))
        phs.append(lambda C, L=L: phase_a2(C, L))
        phs.append(lambda C, L=L: phase_attn(C, L))
        phs.append(lambda C, L=L: phase_sf(C, L))
        phs.append(lambda C, L=L: phase_sb(C, L))
        phs.append(lambda C, L=L, xin=xin: phase_d1(C, L, C.dram[xin]))
        phs.append(lambda C, L=L: phase_d2(C, L, L == DEPTH - 1))
    return phs


_NC_CACHE = {}


def kernel(**inputs):
    inp = {k: np.asarray(v) for k, v in inputs.items()}
    if "nc" not in _NC_CACHE:
        _NC_CACHE["nc"] = build(all_phases())
    nc = _NC_CACHE["nc"]
    cos, sin = rope_tables()
    tri, bfc = fixed_consts()
    consts = pack_consts(inp)
    shared = {
        "w_in": np.ascontiguousarray(inp["w_in"], np.float32), "w_out": np.ascontiguousarray(inp["w_out"], np.float32),
        "w_up": np.ascontiguousarray(inp["w_up"], np.float32), "w_down": np.ascontiguousarray(inp["w_down"], np.float32),
        "consts": consts, "tri": tri, "bfc": bfc, "rope_cos": cos, "rope_sin": sin,
    }
    x = np.ascontiguousarray(inp["x"], np.float32)
    in_maps = [dict(shared, x=x[b]) for b in range(8)]
    res = run_bass_kernel_spmd(nc, in_maps, core_ids=list(range(8)))
    return np.stack([np.asarray(res.results[b]["out"], np.float32) for b in range(8)], axis=0)
```
